# Optimizing a Trainium2 kernel written in Bass

```python
import math
import jax
import jax.numpy as jnp
from jax import lax
import numpy as np

D_MODEL = 4096
BATCH = 1
SEQ = 16384
DEPTH = 4

CHUNK = 64
N_MIXERS = 2
N_SSD_LAYERS = (DEPTH + N_MIXERS - 1) // N_MIXERS
N_DSA_LAYERS = DEPTH // N_MIXERS

SSD_EXPAND = 2
SSD_D_INNER = SSD_EXPAND * D_MODEL
SSD_HEAD_DIM = 64
SSD_N_HEADS = SSD_D_INNER // SSD_HEAD_DIM
SSD_D_STATE = 128
SSD_N_GROUPS = 8
SSD_HEADS_PER_GROUP = SSD_N_HEADS // SSD_N_GROUPS
SSD_CONV = 4
SSD_CHUNK = 128
SSD_CONV_DIM = SSD_D_INNER + 2 * SSD_N_GROUPS * SSD_D_STATE
SSD_IN_DIM = SSD_D_INNER + SSD_CONV_DIM + SSD_N_HEADS

ATT_HEAD_DIM = 128
ATT_N_HEADS = D_MODEL // ATT_HEAD_DIM
ATT_N_KV = 8
ATT_GROUP = ATT_N_HEADS // ATT_N_KV
ATT_WIDTH = ATT_N_HEADS * ATT_HEAD_DIM
IDX_N_HEADS = 64
IDX_HEAD_DIM = 128
TOPK_MAX = 256
Q_BLOCK = 128
DSA_SPLITS = (ATT_WIDTH, ATT_N_KV * ATT_HEAD_DIM, ATT_N_KV * ATT_HEAD_DIM, ATT_WIDTH,
              IDX_N_HEADS * IDX_HEAD_DIM, IDX_HEAD_DIM, IDX_N_HEADS)
DSA_IN_DIM = sum(DSA_SPLITS)

ROPE_THETA = 500000.0
ROPE_DIM_ATT = ATT_HEAD_DIM // 4
ROPE_DIM_IDX = IDX_HEAD_DIM // 4
DEEPNORM_ALPHA = (2 * DEPTH) ** 0.25
DEEPNORM_BETA = (8 * DEPTH) ** -0.25
LN_EPS = 1e-5
RMS_EPS = 1e-5

kernel_name = "hybrid_ssd_dsa_deepnorm_trunk"


def layer_norm(x, g, b):
    xf = x.astype(jnp.float32)
    mu = jnp.mean(xf, axis=-1, keepdims=True)
    xc = xf - mu
    var = jnp.mean(xc * xc, axis=-1, keepdims=True)
    return (xc * lax.rsqrt(var + LN_EPS) * g.astype(jnp.float32) + b.astype(jnp.float32)).astype(x.dtype)


def partial_rope(x, pos, rot_dim):
    half = rot_dim // 2
    inv = jnp.power(ROPE_THETA, -2.0 * jnp.arange(half, dtype=jnp.float32) / rot_dim)
    ang = pos.astype(jnp.float32)[:, None] * inv[None, :]
    cos = jnp.cos(ang)[:, None, :]
    sin = jnp.sin(ang)[:, None, :]
    xf = x.astype(jnp.float32)
    x1 = xf[..., :half]
    x2 = xf[..., half:rot_dim]
    out = jnp.concatenate([x1 * cos - x2 * sin, x2 * cos + x1 * sin, xf[..., rot_dim:]], axis=-1)
    return out.astype(x.dtype)


def causal_dwconv(u, w, b):
    ch = u.shape[-1]
    out = lax.conv_general_dilated(
        u, w[:, None, :].astype(u.dtype), window_strides=(1,), padding=[(w.shape[0] - 1, 0)],
        dimension_numbers=("NWC", "WIO", "NWC"), feature_group_count=ch)
    return out + b.astype(u.dtype)


def ssd_scan(X, A, Bm, Cm):
    b, L, H, P = X.shape
    G, N = Bm.shape[2], Bm.shape[3]
    E = H // G
    Q = SSD_CHUNK
    c = L // Q
    X = X.reshape(b, c, Q, G, E, P)
    A = A.reshape(b, c, Q, G, E).transpose(0, 3, 4, 1, 2)
    Bm = Bm.reshape(b, c, Q, G, N)
    Cm = Cm.reshape(b, c, Q, G, N)
    A_cs = jnp.cumsum(A, axis=-1)
    causal = jnp.tril(jnp.ones((Q, Q), dtype=bool))
    seg = A_cs[..., :, None] - A_cs[..., None, :]
    CB = jnp.einsum('bcqgn,bcsgn->bgcqs', Cm, Bm)
    W = CB[:, :, None] * jnp.exp(jnp.where(causal, seg, -jnp.inf))
    y_diag = jnp.einsum('bgecqs,bcsgep->bcqgep', W, X)
    decay = jnp.exp(A_cs[..., -1:] - A_cs).transpose(0, 3, 4, 1, 2)
    states = jnp.einsum('bcqgn,bcqgep->cbgepn', Bm, X * decay[..., None])
    chunk_decay = jnp.exp(A_cs[..., -1]).transpose(3, 0, 1, 2)

    def step(h, inp):
        s, d = inp
        return d[..., None, None] * h + s, h

    h0 = jnp.zeros((b, G, E, P, N), dtype=X.dtype)
    _, prev = lax.scan(step, h0, (states, chunk_decay))
    in_decay = jnp.exp(A_cs).transpose(0, 3, 4, 1, 2)
    y_off = jnp.einsum('bcqgn,cbgepn->bcqgep', Cm, prev) * in_decay[..., None]
    return (y_diag + y_off).reshape(b, L, H, P)


def ssd_mixer(x, in_w, conv_w, conv_b, dt_bias, a_log, d_skip, norm_g, out_w):
    bsz, L, _ = x.shape
    proj = x @ in_w
    z = proj[..., :SSD_D_INNER]
    xbc = proj[..., SSD_D_INNER:SSD_D_INNER + SSD_CONV_DIM]
    dt_raw = proj[..., SSD_D_INNER + SSD_CONV_DIM:]
    xbc = jax.nn.silu(causal_dwconv(xbc, conv_w, conv_b)).astype(jnp.float32)
    gn = SSD_N_GROUPS * SSD_D_STATE
    xs = xbc[..., :SSD_D_INNER].reshape(bsz, L, SSD_N_HEADS, SSD_HEAD_DIM)
    Bm = xbc[..., SSD_D_INNER:SSD_D_INNER + gn].reshape(bsz, L, SSD_N_GROUPS, SSD_D_STATE)
    Cm = xbc[..., SSD_D_INNER + gn:].reshape(bsz, L, SSD_N_GROUPS, SSD_D_STATE)
    dt = jax.nn.softplus(dt_raw.astype(jnp.float32) + dt_bias.astype(jnp.float32))
    A = -jnp.exp(a_log.astype(jnp.float32))
    y = ssd_scan(xs * dt[..., None], dt * A, Bm, Cm)
    y = y + d_skip.astype(jnp.float32)[:, None] * xs
    y = y.reshape(bsz, L, SSD_D_INNER) * jax.nn.silu(z.astype(jnp.float32))
    yg = y.reshape(bsz, L, SSD_N_GROUPS, SSD_D_INNER // SSD_N_GROUPS)
    yg = yg * lax.rsqrt(jnp.mean(yg * yg, axis=-1, keepdims=True) + RMS_EPS)
    y = yg.reshape(bsz, L, SSD_D_INNER) * norm_g.astype(jnp.float32)
    return y.astype(x.dtype) @ out_w


def dsa_mixer(x, in_w, kn_g, kn_b, out_w):
    bsz, L, _ = x.shape
    proj = x @ in_w
    cuts = [int(c) for c in np.cumsum(DSA_SPLITS)[:-1]]
    q, k, v, z, qi, ki, wi = jnp.split(proj, cuts, axis=-1)
    pos = jnp.arange(L, dtype=jnp.int32)
    q = partial_rope(q.reshape(bsz, L, ATT_N_HEADS, ATT_HEAD_DIM), pos, ROPE_DIM_ATT)
    k = partial_rope(k.reshape(bsz, L, ATT_N_KV, ATT_HEAD_DIM), pos, ROPE_DIM_ATT)
    v = v.reshape(bsz, L, ATT_N_KV, ATT_HEAD_DIM)
    qi = partial_rope(qi.reshape(bsz, L, IDX_N_HEADS, IDX_HEAD_DIM), pos, ROPE_DIM_IDX)
    ki = partial_rope(layer_norm(ki, kn_g, kn_b)[:, :, None, :], pos, ROPE_DIM_IDX)[:, :, 0, :]
    ki = ki.astype(jnp.float32)
    wi = wi.astype(jnp.float32) * (IDX_N_HEADS ** -0.5 * IDX_HEAD_DIM ** -0.5)
    topk = min(TOPK_MAX, L // 4)
    key_chunk = pos // CHUNK
    scale = ATT_HEAD_DIM ** -0.5
    gather = jax.vmap(lambda arr, idx: arr[idx])
    nb = L // Q_BLOCK

    def to_blocks(a):
        return a.reshape((bsz, nb, Q_BLOCK) + a.shape[2:]).swapaxes(0, 1)

    def body(blk):
        qb, qib, wb, tb = blk
        q_chunk = tb // CHUNK
        rel = jax.nn.relu(jnp.einsum('bqhd,bsd->bqhs', qib.astype(jnp.float32), ki))
        score = jnp.einsum('bqhs,bqh->bqs', rel, wb)
        adm = key_chunk[None, :] <= q_chunk[:, None]
        score = jnp.where(adm[None], score, -jnp.inf)
        _, idx = lax.top_k(score, topk)
        valid = (idx // CHUNK) <= q_chunk[None, :, None]
        k_sel = gather(k, idx)
        v_sel = gather(v, idx)
        qg = qb.reshape(bsz, Q_BLOCK, ATT_N_KV, ATT_GROUP, ATT_HEAD_DIM)
        logits = jnp.einsum('bqhgd,bqkhd->bqhgk', qg, k_sel).astype(jnp.float32) * scale
        logits = jnp.where(valid[:, :, None, None, :], logits, -jnp.inf)
        p = jax.nn.softmax(logits, axis=-1).astype(v.dtype)
        o = jnp.einsum('bqhgk,bqkhd->bqhgd', p, v_sel)
        return o.reshape(bsz, Q_BLOCK, ATT_WIDTH)

    o = lax.map(body, (to_blocks(q), to_blocks(qi), to_blocks(wi), pos.reshape(nb, Q_BLOCK)))
    o = o.swapaxes(0, 1).reshape(bsz, L, ATT_WIDTH)
    o = o * jax.nn.silu(z)
    return o @ out_w


def setup_inputs(seed: int = 0) -> dict:
    key = jax.random.key(seed)
    ks = jax.random.split(key, 16)
    f32 = jnp.float32
    x = jax.random.normal(ks[0], (BATCH, SEQ, D_MODEL), f32)
    ssd_in_w = jax.random.normal(ks[1], (N_SSD_LAYERS, D_MODEL, SSD_IN_DIM), f32) * D_MODEL ** -0.5
    ssd_conv_w = jax.random.normal(ks[2], (N_SSD_LAYERS, SSD_CONV, SSD_CONV_DIM), f32) * SSD_CONV ** -0.5
    ssd_conv_b = 0.01 * jax.random.normal(ks[3], (N_SSD_LAYERS, SSD_CONV_DIM), f32)
    dt0 = jnp.exp(jax.random.uniform(ks[4], (N_SSD_LAYERS, SSD_N_HEADS), f32,
                                     math.log(1e-3), math.log(1e-1)))
    ssd_dt_bias = dt0 + jnp.log(-jnp.expm1(-dt0))
    ssd_a_log = jnp.log(jax.random.uniform(ks[5], (N_SSD_LAYERS, SSD_N_HEADS), f32, 1.0, 16.0))
    ssd_d_skip = 1.0 + 0.01 * jax.random.normal(ks[6], (N_SSD_LAYERS, SSD_N_HEADS), f32)
    ssd_norm_g = 1.0 + 0.01 * jax.random.normal(ks[7], (N_SSD_LAYERS, SSD_D_INNER), f32)
    ssd_out_w = jax.random.normal(ks[8], (N_SSD_LAYERS, SSD_D_INNER, D_MODEL), f32) * (SSD_D_INNER ** -0.5 * DEEPNORM_BETA)
    dsa_in_w = jax.random.normal(ks[9], (N_DSA_LAYERS, D_MODEL, DSA_IN_DIM), f32) * D_MODEL ** -0.5
    dsa_kn_g = 1.0 + 0.01 * jax.random.normal(ks[10], (N_DSA_LAYERS, IDX_HEAD_DIM), f32)
    dsa_kn_b = 0.01 * jax.random.normal(ks[11], (N_DSA_LAYERS, IDX_HEAD_DIM), f32)
    dsa_out_w = jax.random.normal(ks[12], (N_DSA_LAYERS, ATT_WIDTH, D_MODEL), f32) * (ATT_WIDTH ** -0.5 * DEEPNORM_BETA)
    ln_g = 1.0 + 0.01 * jax.random.normal(ks[13], (DEPTH, D_MODEL), f32)
    ln_b = 0.01 * jax.random.normal(ks[14], (DEPTH, D_MODEL), f32)
    return {"x": x, "ssd_in_w": ssd_in_w, "ssd_conv_w": ssd_conv_w, "ssd_conv_b": ssd_conv_b,
            "ssd_dt_bias": ssd_dt_bias, "ssd_a_log": ssd_a_log, "ssd_d_skip": ssd_d_skip,
            "ssd_norm_g": ssd_norm_g, "ssd_out_w": ssd_out_w, "dsa_in_w": dsa_in_w,
            "dsa_kn_g": dsa_kn_g, "dsa_kn_b": dsa_kn_b, "dsa_out_w": dsa_out_w,
            "ln_g": ln_g, "ln_b": ln_b}


def reference(x, ssd_in_w, ssd_conv_w, ssd_conv_b, ssd_dt_bias, ssd_a_log, ssd_d_skip,
              ssd_norm_g, ssd_out_w, dsa_in_w, dsa_kn_g, dsa_kn_b, dsa_out_w, ln_g, ln_b):
    for i in range(DEPTH):
        j = i // N_MIXERS
        if i % N_MIXERS == 0:
            f = ssd_mixer(x, ssd_in_w[j], ssd_conv_w[j], ssd_conv_b[j], ssd_dt_bias[j],
                          ssd_a_log[j], ssd_d_skip[j], ssd_norm_g[j], ssd_out_w[j])
        else:
            f = dsa_mixer(x, dsa_in_w[j], dsa_kn_g[j], dsa_kn_b[j], dsa_out_w[j])
        x = layer_norm(DEEPNORM_ALPHA * x + f, ln_g[i], ln_b[i])
    return x
```

```python
import numpy as np
import ml_dtypes
from contextlib import ExitStack
import concourse.bass as bass
import concourse.mybir as mybir
from concourse.bass_utils import run_bass_kernel_spmd

F32 = mybir.dt.float32
BF16 = mybir.dt.bfloat16
AF = mybir.ActivationFunctionType
ALU = mybir.AluOpType
AX = mybir.AxisListType
NCORE = 8
D = 4096
DI = 8192
NH = 128
HP = 64
NG = 8
NS_ = 128
SSD_IN = 18560
DSA_IN = 18624
ALPHA = 8.0 ** 0.25
LN_EPS = 1e-5
RMS_EPS = 1e-5
NEG = -30000.0
TOPK = 256
ROPE_THETA = 500000.0


class Res:
    __slots__ = ("w", "rc", "rd")

    def __init__(self):
        self.w = None
        self.rc = {}
        self.rd = []


class Sched:
    ENG = ("pe", "act", "dve", "pool", "sp")
    NSD = 16

    def __init__(self, nc, stack):
        self.nc = nc
        self.ops = {e: [] for e in self.ENG}
        self.handle = {}
        self.ccount = {e: 0 for e in self.ENG}
        self.dcount = {e: 0 for e in self.ENG}
        self.kcount = 0
        self.ksem = [stack.enter_context(nc.semaphore("k%d" % i)) for i in range(4)]
        self.csem = {e: stack.enter_context(nc.semaphore("c_" + e)) for e in ("pe", "act", "dve", "pool")}
        self.dsem = {q: [stack.enter_context(nc.semaphore("d_%s%d" % (q, i))) for i in range(self.NSD)]
                     for q in ("sp", "pool")}
        self.emitted = {e: 0 for e in self.ENG}
        self.waited = {e: {} for e in self.ENG}
        self.dlast = {q: [0] * self.NSD for q in ("sp", "pool")}
        self.klast = [0] * 4

    def add(self, eng, fn, rd=(), wr=(), kind="c"):
        ops = self.ops[eng]
        me = (eng, len(ops))
        deps = set()
        for r in rd:
            if r.w is not None:
                deps.add(r.w)
        for r in wr:
            if r.w is not None:
                deps.add(r.w)
            for e2, i2 in r.rc.items():
                deps.add((e2, i2))
            deps.update(r.rd)
        deps.discard(me)
        for r in rd:
            if kind == "c":
                r.rc[eng] = me[1]
            else:
                r.rd.append(me)
        for r in wr:
            r.w = me
            r.rc = {}
            r.rd = []
        if kind == "c":
            self.ccount[eng] += 1
            self.handle[me] = ("c", eng, self.ccount[eng])
        elif kind == "d":
            n = self.dcount[eng]
            self.dcount[eng] += 1
            self.handle[me] = ("d", eng, n % self.NSD, 16 * (n // self.NSD + 1))
        else:
            n = self.kcount
            self.kcount += 1
            self.handle[me] = ("k", eng, n % 4, n // 4 + 1)
        ops.append((fn, deps, kind))
        return me

    def barrier(self):
        for e in self.ENG:
            self.ops[e].append((None, None, "bar"))
        self._bar_state = None

    def _sem_of(self, h):
        if h[0] == "c":
            return self.csem[h[1]], h[2]
        if h[0] == "d":
            return self.dsem[h[1]][h[2]], h[3]
        return self.ksem[h[2]], h[3]

    def flush(self, only=None):
        nc = self.nc
        self.barrier()
        snaps = self._snapshots()
        engobj = {"pe": "tensor", "act": "scalar", "dve": "vector", "pool": "gpsimd", "sp": "sync"}
        with nc.Block() as block:
            for ename in self.ENG:
                if only is not None and ename not in only:
                    assert all(o[2] == "bar" for o in self.ops[ename][self.emitted[ename]:]), ename
                    continue
                def body(eobj, ename=ename):
                    self._emit_engine(ename, eobj, snaps)
                getattr(block, engobj[ename])(body)
        for e in self.ENG:
            self.emitted[e] = len(self.ops[e])
        if only is not None:
            cc, dl, kl = snaps[-1]
            for e in self.ENG:
                w = self.waited[e]
                for e2 in ("pe", "act", "dve", "pool"):
                    w[id(self.csem[e2])] = max(w.get(id(self.csem[e2]), 0), cc[e2])
                for q in ("sp", "pool"):
                    for i in range(self.NSD):
                        w[id(self.dsem[q][i])] = max(w.get(id(self.dsem[q][i]), 0), dl[q][i])
                for i in range(4):
                    w[id(self.ksem[i])] = max(w.get(id(self.ksem[i]), 0), kl[i])

    def _snapshots(self):
        snaps = []
        pos = {e: 0 for e in self.ENG}
        cc = {e: 0 for e in self.ENG}
        dl = {q: [0] * self.NSD for q in ("sp", "pool")}
        kl = [0] * 4
        nbar = sum(1 for o in self.ops["pe"] if o[2] == "bar")
        for _ in range(nbar):
            for e in self.ENG:
                ops = self.ops[e]
                i = pos[e]
                while ops[i][2] != "bar":
                    h = self.handle[(e, i)]
                    if h[0] == "c":
                        cc[e] = h[2]
                    elif h[0] == "d":
                        dl[e][h[2]] = h[3]
                    else:
                        kl[h[2]] = h[3]
                    i += 1
                pos[e] = i + 1
            snaps.append(({e: cc[e] for e in cc}, {q: list(dl[q]) for q in dl}, list(kl)))
        return snaps

    def _wait(self, ename, eobj, sem, val):
        w = self.waited[ename]
        key = id(sem)
        if w.get(key, 0) >= val:
            return
        eobj.wait_ge(sem, val)
        w[key] = val

    def _emit_engine(self, ename, eobj, snaps):
        ops = self.ops[ename]
        nbar_before = sum(1 for o in ops[:self.emitted[ename]] if o[2] == "bar")
        bi = nbar_before
        for i in range(self.emitted[ename], len(ops)):
            fn, deps, kind = ops[i]
            if kind == "bar":
                cc, dl, kl = snaps[bi]
                bi += 1
                for e2 in ("pe", "act", "dve", "pool"):
                    if cc[e2] > 0:
                        self._wait(ename, eobj, self.csem[e2], cc[e2])
                for q in ("sp", "pool"):
                    for s in range(self.NSD):
                        if dl[q][s] > 0:
                            self._wait(ename, eobj, self.dsem[q][s], dl[q][s])
                for s in range(4):
                    if kl[s] > 0:
                        self._wait(ename, eobj, self.ksem[s], kl[s])
                continue
            for d in sorted(deps):
                if ename == "pe" and d[0] == "pe":
                    continue
                sem, val = self._sem_of(self.handle[d])
                self._wait(ename, eobj, sem, val)
            h = self.handle[(ename, i)]
            if h[0] == "d":
                prev = h[3] - 16
                if prev > 0:
                    self._wait(ename, eobj, self.dsem[ename][h[2]], prev)
            elif h[0] == "k":
                prev = h[3] - 1
                if prev > 0:
                    self._wait(ename, eobj, self.ksem[h[2]], prev)
            ins = fn(eobj)
            sem, val = self._sem_of(h)
            ins.then_inc(sem, 16 if h[0] == "d" else 1)


class Rot:
    def __init__(self, tiles):
        self.t = [(t, Res()) for t in tiles]
        self.i = 0

    def next(self):
        r = self.t[self.i % len(self.t)]
        self.i += 1
        return r


class Cfg:
    def __init__(self, L=16384, layers=(0, 1, 0, 1)):
        self.L = L
        self.T = L // NCORE
        self.TT = self.T // 128
        self.layers = layers
        self.TB = min(self.T, 1024)
        self.KC = min(self.T, 512)
        self.coll = True


def op_mm(S, out, lhsT, rhs, start, stop, rd, wr):
    S.add("pe", lambda e: e.matmul(out, lhsT, rhs, start=start, stop=stop), rd, wr)


def op_tr(S, out, in_, ident, rd, wr):
    S.add("pe", lambda e: e.transpose(out, in_, ident), rd, wr)


def op_act(S, out, in_, func, rd, wr, bias=0.0, scale=1.0, accum=None):
    if accum is None:
        S.add("act", lambda e: e.activation(out, in_, func, bias=bias, scale=scale), rd, wr)
    else:
        S.add("act", lambda e: e.activation(out, in_, func, bias=bias, scale=scale, accum_out=accum), rd, wr)


def op_copy(S, eng, out, in_, rd, wr):
    if eng == "act":
        S.add("act", lambda e: e.copy(out, in_), rd, wr)
    else:
        S.add(eng, lambda e: e.tensor_copy(out, in_), rd, wr)


def op_tt(S, eng, out, in0, in1, op, rd, wr):
    S.add(eng, lambda e: e.tensor_tensor(out, in0, in1, op), rd, wr)


def op_ts(S, eng, out, in0, s1, s2, op0, op1, rd, wr, accum=None):
    if accum is None:
        if s2 is None:
            S.add(eng, lambda e: e.tensor_scalar(out, in0, s1, None, op0), rd, wr)
        else:
            S.add(eng, lambda e: e.tensor_scalar(out, in0, s1, s2, op0, op1), rd, wr)
    else:
        S.add(eng, lambda e: e.tensor_scalar(out, in0, s1, s2, op0, op1, accum_out=accum), rd, wr)


def op_stt(S, out, in0, scalar, in1, op0, op1, rd, wr):
    S.add("dve", lambda e: e.scalar_tensor_tensor(out, in0, scalar, in1, op0, op1), rd, wr)


def op_dma(S, out, in_, rd, wr, q="sp"):
    S.add(q, lambda e: e.dma_start(out=out, in_=in_), rd, wr, kind="d")


def op_memset(S, eng, ap, val, wr):
    S.add(eng, lambda e: e.memset(ap, val), (), wr)


class Prog:
    def __init__(self, cfg):
        self.cfg = cfg
        self.nc = bass.Bass("TRN2", target_bir_lowering=False)
        self.stack = ExitStack()
        self.S = Sched(self.nc, self.stack)
        self.dr = {}
        self.res = {}
        self.wjobs = []

    def din(self, name, shape, dt=F32):
        t = self.nc.dram_tensor(name, list(shape), dt, kind="ExternalInput").ap()
        self.dr[name] = t
        return t

    def dout(self, name, shape, dt=F32):
        t = self.nc.dram_tensor(name, list(shape), dt, kind="ExternalOutput").ap()
        self.dr[name] = t
        return t

    def dtmp(self, name, shape, dt=F32):
        if name in getattr(self.cfg, "ext_in", ()):
            return self.din(name, shape, dt)
        if name in getattr(self.cfg, "ext_out", ()):
            return self.dout(name, shape, dt)
        t = self.nc.dram_tensor(name, list(shape), dt).ap()
        self.dr[name] = t
        return t


def gather_weight(P, name, K, N):
    nc, S = P.nc, P.S
    ks = K // NCORE if (P.cfg.coll or getattr(P.cfg, "wcoll", False)) else K
    src = P.din(name, [ks, N])
    sh = P.dtmp(name + "_sh", [ks, N], BF16)
    full = P.dtmp(name + "_bf", [K, N], BF16)
    r = Res()
    P.res[name] = r
    P.wjobs.append((name, src, sh, full, ks, N, r))
    return full if (P.cfg.coll or getattr(P.cfg, "wcoll", False)) else sh


def phase_weights_pool(P):
    nc, S = P.nc, P.S
    with ExitStack() as st:
        f_r = Rot([sbt(st, nc, "wp_f%d" % i, [128, 4096], F32) for i in range(2)])
        b_r = Rot([sbt(st, nc, "wp_b%d" % i, [128, 4096], BF16) for i in range(2)])
        for (name, src, sh, full, ks, N, r) in P.wjobs:
            for r0 in range(0, ks, 128):
                r1 = min(ks, r0 + 128)
                pr = r1 - r0
                for c0 in range(0, N, 4096):
                    c1 = min(N, c0 + 4096)
                    f, rf = f_r.next()
                    b, rb = b_r.next()
                    op_dma(S, f[:pr, :c1 - c0], src[r0:r1, c0:c1], (), (rf,), q="pool")
                    op_copy(S, "pool", b[:pr, :c1 - c0], f[:pr, :c1 - c0], (rf,), (rb,))
                    op_dma(S, sh[r0:r1, c0:c1], b[:pr, :c1 - c0], (rb,), (r,), q="pool")
            P.S.add("pool", lambda e, a=sh[:, :], bb=full[:, :]: e.collective_compute(
                "AllGather", ALU.bypass, replica_groups=[list(range(NCORE))], ins=[a], outs=[bb]), (r,), (r,), kind="k")
        S.flush(only=("pool",))
    for (name, src, sh, full, ks, N, r) in P.wjobs:
        r.w = None
        r.rc = {}
        r.rd = []


def phase_weights(P):
    if getattr(P.cfg, "wcoll", False):
        return phase_weights_pool(P)
    nc, S = P.nc, P.S
    with ExitStack() as st:
        f_r = Rot([sbt(st, nc, "w_f%d" % i, [128, 4096], F32) for i in range(3)])
        b_r = Rot([sbt(st, nc, "w_b%d" % i, [128, 4096], BF16) for i in range(3)])
        n = 0
        for (name, src, sh, full, ks, N, r) in P.wjobs:
            for r0 in range(0, ks, 128):
                r1 = min(ks, r0 + 128)
                pr = r1 - r0
                for c0 in range(0, N, 4096):
                    c1 = min(N, c0 + 4096)
                    f, rf = f_r.next()
                    b, rb = b_r.next()
                    op_dma(S, f[:pr, :c1 - c0], src[r0:r1, c0:c1], (), (rf,))
                    op_copy(S, ("pool", "act", "dve")[n % 3], b[:pr, :c1 - c0], f[:pr, :c1 - c0], (rf,), (rb,))
                    op_dma(S, sh[r0:r1, c0:c1], b[:pr, :c1 - c0], (rb,), ())
                    n += 1
        S.flush()
    for (name, src, sh, full, ks, N, r) in P.wjobs:
        if P.cfg.coll:
            allgather(P, sh[:, :], full[:, :], r)


def allgather(P, src, dst, res):
    if getattr(P.cfg, "multi", False):
        return
    if not P.cfg.coll:
        rows = src.shape[0]
        for r in range(NCORE):
            op_dma(P.S, dst[r * rows:(r + 1) * rows, :], src, (), (res,))
        return
    P.S.add("pool", lambda e: e.collective_compute("AllGather", ALU.bypass, replica_groups=[list(range(NCORE))],
                                                    ins=[src], outs=[dst]), (), (res,), kind="k")


_UID = [0]


def sbt(st, nc, name, shape, dt):
    _UID[0] += 1
    return st.enter_context(nc.sbuf_tensor("%s_%d" % (name, _UID[0]), list(shape), dt))


def pst(st, nc, name, shape, dt):
    _UID[0] += 1
    return st.enter_context(nc.psum_tensor("%s_%d" % (name, _UID[0]), list(shape), dt))


def setup_consts(P):
    nc, S, st = P.nc, P.S, P.stack
    c = {}
    P.c = c
    c["ident_f"] = sbt(st, nc, "ident_f", [128, 128], F32)
    c["ident_b"] = sbt(st, nc, "ident_b", [128, 128], BF16)
    c["utri"] = sbt(st, nc, "utri", [128, 128], F32)
    c["ones_f"] = sbt(st, nc, "ones_f", [128, 128], F32)
    c["ones_b"] = sbt(st, nc, "ones_b", [128, 128], BF16)
    c["negm4"] = sbt(st, nc, "negm4", [128, 512], F32)
    c["utri_b"] = sbt(st, nc, "utri_b", [128, 128], BF16)
    c["negm4_b"] = sbt(st, nc, "negm4_b", [128, 512], BF16)
    c["mrank"] = sbt(st, nc, "mrank", [128, 2 * NCORE], F32)
    c["sel24"] = sbt(st, nc, "sel24", [3 * NCORE, 3], BF16)
    c["sel24f"] = sbt(st, nc, "sel24f", [3 * NCORE, 3], F32)
    r = Res()
    c["res"] = r
    d_ident = P.din("c_ident", [128, 128])
    d_utri = P.din("c_utri", [128, 128])
    d_negm = P.din("c_negm4", [128, 512])
    d_mrank = P.din("c_mrank", [128, 2 * NCORE])
    d_sel = P.din("c_sel24", [3 * NCORE, 3])
    op_dma(S, c["ident_f"][:, :], d_ident[:, :], (), (r,))
    op_dma(S, c["utri"][:, :], d_utri[:, :], (), (r,))
    op_dma(S, c["negm4"][:, :], d_negm[:, :], (), (r,))
    op_dma(S, c["mrank"][:, :], d_mrank[:, :], (), (r,))
    op_dma(S, c["sel24f"][:, :], d_sel[:, :], (), (r,))
    op_copy(S, "dve", c["ident_b"][:, :], c["ident_f"][:, :], (r,), (r,))
    op_copy(S, "dve", c["sel24"][:, :], c["sel24f"][:, :], (r,), (r,))
    op_copy(S, "dve", c["utri_b"][:, :], c["utri"][:, :], (r,), (r,))
    op_copy(S, "dve", c["negm4_b"][:, :], c["negm4"][:, :], (r,), (r,))
    op_memset(S, "dve", c["ones_f"][:, :], 1.0, (r,))
    op_memset(S, "dve", c["ones_b"][:, :], 1.0, (r,))
    S.flush()


def emit_xT(P, st_bufs, xt, rx, t):
    S, c, cfg = P.S, P.c, P.cfg
    xb_rot, ps_rot, xTt_rot = st_bufs
    xb, rxb = xb_rot.next()
    op_copy(S, "act", xb[:, :], xt[:, :], (rx,), (rxb,))
    xTt, rxt = xTt_rot.next()
    for g in range(4):
        ps, rps = ps_rot.next()
        for i in range(8):
            kt = g * 8 + i
            op_tr(S, ps[:, i, :], xb[:, kt * 128:(kt + 1) * 128], c["ident_b"][:, :], (rxb, c["res"]), (rps,))
        op_copy(S, "dve", xTt[:, g * 8:(g + 1) * 8, :], ps[:, :, :], (rps,), (rxt,))
    xTd = P.dr["xT"].rearrange("(kt p) t -> p kt t", p=128)
    op_dma(S, xTd[:, :, t * 128:(t + 1) * 128], xTt[:, :, :], (rxt,), ())
    if t == cfg.TT - 1:
        op_dma(S, P.dr["tail"][0:3, :], xb[125:128, :], (rxb,), ())


def phase_prologue(P):
    nc, S, cfg = P.nc, P.S, P.cfg
    x = P.dr["x"]
    with ExitStack() as st:
        xt_rot = Rot([sbt(st, nc, "pl_xt%d" % i, [128, D], F32) for i in range(2)])
        bufs = (Rot([sbt(st, nc, "pl_xb%d" % i, [128, D], BF16) for i in range(2)]),
                Rot([pst(st, nc, "pl_ps%d" % i, [128, 8, 128], BF16) for i in range(4)]),
                Rot([sbt(st, nc, "pl_xTt%d" % i, [128, 32, 128], BF16) for i in range(2)]))
        for t in range(cfg.TT):
            xt, rx = xt_rot.next()
            op_dma(S, xt[:, :], x[t * 128:(t + 1) * 128, :], (), (rx,))
            emit_xT(P, bufs, xt, rx, t)
        S.flush()
    allgather(P, P.dr["tail"][:, :], P.dr["tails"][:, :], P.res["tails"])


def ssd_phaseA(P, j, W, col_start=0):
    nc, S, cfg, c, dr = P.nc, P.S, P.cfg, P.c, P.dr
    T, TB = cfg.T, cfg.TB
    NB, NTB = T // TB, TB // 128
    NCH = (TB + 511) // 512
    CW = TB // NCH
    Wd = W.rearrange("(kt p) n -> p kt n", p=128)
    xTd = dr["xT"].rearrange("(kt p) t -> p kt t", p=128)
    rW = P.res["ssd_in_w%d" % j]
    with ExitStack() as st:
        xTb = sbt(st, nc, "a_xTb", [128, 32, TB + 3], BF16)
        rxTb = Res()
        Wg = Rot([sbt(st, nc, "a_Wg%d" % i, [128, 32, 512], BF16) for i in range(2)])
        tails = sbt(st, nc, "a_tails", [3 * NCORE, D], BF16)
        rtl = Res()
        cw = sbt(st, nc, "a_cw", [128, 80, 4], F32)
        cb = sbt(st, nc, "a_cb", [128, 80], F32)
        dtb = sbt(st, nc, "a_dtb", [128, 128], F32)
        Ab = sbt(st, nc, "a_Ab", [128, 128], F32)
        rcst = Res()
        u_rot = Rot([sbt(st, nc, "a_u%d" % i, [128, TB + 3], F32) for i in range(2)])
        acc_rot = Rot([sbt(st, nc, "a_acc%d" % i, [128, TB], F32) for i in range(2)])
        v_rot = Rot([sbt(st, nc, "a_v%d" % i, [128, TB], F32) for i in range(2)])
        vb_rot = Rot([sbt(st, nc, "a_vb%d" % i, [128, TB], BF16) for i in range(4)])
        zs_rot = Rot([sbt(st, nc, "a_zs%d" % i, [128, 512], F32) for i in range(2)])
        xf_rot = Rot([sbt(st, nc, "a_xf%d" % i, [128, NTB, 128], F32) for i in range(2)])
        xb_rot = Rot([sbt(st, nc, "a_xb%d" % i, [128, NTB, 128], BF16) for i in range(2)])
        sm_rot = Rot([sbt(st, nc, "a_sm%d" % i, [128, 128], F32) for i in range(6)])
        psA = Rot([pst(st, nc, "a_psA%d" % i, [128, 512], F32) for i in range(2 if NCH == 1 else 1)])
        psH = pst(st, nc, "a_psH", [128, 32, 4], F32)
        rpsH = Res()
        psU = Rot([pst(st, nc, "a_psU%d" % i, [128, NCH, 512], F32) for i in range(2)]) if NCH == 2 else \
            Rot([pst(st, nc, "a_psU%d" % i, [128, NCH, 512], F32) for i in range(3)])
        psB = Rot([pst(st, nc, "a_psB%d" % i, [128, 8, 128], BF16) for i in range(2)])
        op_dma(S, cw[:, :, :], dr["ssd_cw%d" % j][:, :, :], (), (rcst,))
        op_dma(S, cb[:, :], dr["ssd_cb%d" % j][:, :], (), (rcst,))
        op_dma(S, dtb[:, :], dr["ssd_dtb%d" % j][:, :], (), (rcst,))
        op_dma(S, Ab[:, :], dr["ssd_alog%d" % j][:, :], (), (rcst,))
        op_act(S, Ab[:, :], Ab[:, :], AF.Exp, (rcst,), (rcst,))
        op_ts(S, "dve", Ab[:, :], Ab[:, :], -1.0, None, ALU.mult, None, (rcst,), (rcst,))
        if getattr(cfg, "multi", False):
            tailsf = sbt(st, nc, "a_tailsf", [3 * NCORE, 512], F32)
            rtf = Res()
            for q4 in range(D // 512):
                op_dma(S, tailsf[:, :], dr["tails_in"][:, q4 * 512:(q4 + 1) * 512], (), (rtf,))
                op_copy(S, "dve", tails[:, q4 * 512:(q4 + 1) * 512], tailsf[:, :], (rtf,), (rtl,))
        else:
            op_dma(S, tails[:, :], dr["tails"][:, :], (P.res["tails"],), (rtl,))
        for b in range(NB):
            op_dma(S, xTb[:, :, 3:], xTd[:, :, b * TB:(b + 1) * TB], (), (rxTb,))
            if b == 0:
                for kt in range(32):
                    op_mm(S, psH[:, kt, 0:3], tails[:, kt * 128:(kt + 1) * 128], c["sel24"][:, :], True, True,
                          (rtl, c["res"]), (rpsH,))
                op_copy(S, "dve", xTb[:, :, 0:3], psH[:, :, 0:3], (rpsH,), (rxTb,))
            else:
                with nc.allow_non_contiguous_dma(reason="3-col halo"):
                    op_dma(S, xTb[:, :, 0:3], xTd[:, :, b * TB - 3:b * TB], (), (rxTb,))
            col = col_start
            import os
            skip = os.environ.get("ASKIP", "")
            while col < SSD_IN:
                ncol = min(512, SSD_IN - col)
                kindc = "z" if col < DI else ("x" if col < DI + 8192 else ("b" if col < DI + 9216 else ("c" if col < DI + 10240 else "d")))
                if kindc in skip:
                    col += ncol
                    continue
                Wt, rWt = Wg.next()
                op_dma(S, Wt[:, :, :ncol], Wd[:, :, col - col_start:col - col_start + ncol], (rW,), (rWt,))
                if col < DI:
                    for tt in range(NTB):
                        ps, rps = psA.next()
                        for kt in range(32):
                            op_mm(S, ps[:, :], xTb[:, kt, 3 + tt * 128:3 + (tt + 1) * 128], Wt[:, kt, :],
                                  kt == 0, kt == 31, (rxTb, rWt), (rps,))
                        zt, rz = zs_rot.next()
                        op_act(S, zt[:, :], ps[:, :], AF.Silu, (rps,), (rz,))
                        r0 = b * TB + tt * 128
                        op_dma(S, dr["zs"][r0:r0 + 128, col:col + 512], zt[:, :], (rz,), ())
                elif col < DI + 10240:
                    for s in range(ncol // 128):
                        ct = (col - DI) // 128 + s
                        pu, rpu = psU.next()
                        u, ru = u_rot.next()
                        for ch in range(NCH):
                            for kt in range(32):
                                op_mm(S, pu[:, ch, :CW], Wt[:, kt, s * 128:(s + 1) * 128],
                                      xTb[:, kt, 3 + ch * CW:3 + (ch + 1) * CW], kt == 0, kt == 31, (rxTb, rWt), (rpu,))
                        ph, rph = psA.next()
                        for kt in range(32):
                            op_mm(S, ph[:, 0:3], Wt[:, kt, s * 128:(s + 1) * 128], xTb[:, kt, 0:3],
                                  kt == 0, kt == 31, (rxTb, rWt), (rph,))
                        op_copy(S, "act", u[:, 0:3], ph[:, 0:3], (rph,), (ru,))
                        op_copy(S, "act", u[:, 3:].rearrange("p (c w) -> p c w", c=NCH), pu[:, :, :CW], (rpu,), (ru,))
                        acc, racc = acc_rot.next()
                        op_ts(S, "dve", acc[:, :], u[:, 0:TB], cw[:, ct, 0:1], None, ALU.mult, None, (ru, rcst), (racc,))
                        for k in range(1, 4):
                            op_stt(S, acc[:, :], u[:, k:k + TB], cw[:, ct, k:k + 1], acc[:, :], ALU.mult, ALU.add,
                                   (ru, rcst, racc), (racc,))
                        v, rv = v_rot.next()
                        op_act(S, v[:, :], acc[:, :], AF.Silu, (racc, rcst), (rv,), bias=cb[:, ct:ct + 1])
                        if ct < 64:
                            xf, rxf = xf_rot.next()
                            xb, rxb = xb_rot.next()
                            vh, rvh = vb_rot.next()
                            op_copy(S, "dve", vh[:, :], v[:, :], (rv,), (rvh,))
                            vl, rvl = vb_rot.next()
                            op_tt(S, "dve", vl[:, :], v[:, :], vh[:, :], ALU.subtract, (rv, rvh), (rvl,))
                            for tt in range(NTB):
                                pb, rpb = psB.next()
                                op_tr(S, pb[:, 0, :], vh[:, tt * 128:(tt + 1) * 128], c["ident_b"][:, :], (rvh, c["res"]), (rpb,))
                                op_tr(S, pb[:, 1, :], vl[:, tt * 128:(tt + 1) * 128], c["ident_b"][:, :], (rvl, c["res"]), (rpb,))
                                op_copy(S, "act", xb[:, tt, :], pb[:, 0, :], (rpb,), (rxb,))
                                op_tt(S, "dve", xf[:, tt, :], pb[:, 1, :], xb[:, tt, :], ALU.add, (rpb, rxb), (rxf,))
                            r0 = b * TB
                            op_dma(S, dr["Xf"][r0:r0 + TB, ct * 128:(ct + 1) * 128].rearrange("(tt p) c -> p tt c", p=128),
                                   xf[:, :, :], (rxf,), ())
                            op_dma(S, dr["Xb"][r0:r0 + TB, ct * 128:(ct + 1) * 128].rearrange("(tt p) c -> p tt c", p=128),
                                   xb[:, :, :], (rxb,), ())
                        else:
                            vb, rvb = vb_rot.next()
                            op_copy(S, "dve", vb[:, :], v[:, :], (rv,), (rvb,))
                            g = (ct - 64) % 8
                            dst = dr["BT"] if ct < 72 else dr["CT"]
                            op_dma(S, dst[g * 128:(g + 1) * 128, b * TB:(b + 1) * TB], vb[:, :], (rvb,), ())
                            if ct < 72:
                                xb, rxb = xb_rot.next()
                                for tt in range(NTB):
                                    pb, rpb = psB.next()
                                    op_tr(S, pb[:, 0, :], vb[:, tt * 128:(tt + 1) * 128], c["ident_b"][:, :], (rvb, c["res"]), (rpb,))
                                    op_copy(S, "dve", xb[:, tt, :], pb[:, 0, :], (rpb,), (rxb,))
                                r0 = b * TB
                                op_dma(S, dr["Bb"][r0:r0 + TB, g * 128:(g + 1) * 128].rearrange("(tt p) c -> p tt c", p=128),
                                       xb[:, :, :], (rxb,), ())
                else:
                    for tt in range(NTB):
                        ps, rps = psA.next()
                        for kt in range(32):
                            op_mm(S, ps[:, 0:128], xTb[:, kt, 3 + tt * 128:3 + (tt + 1) * 128], Wt[:, kt, 0:128],
                                  kt == 0, kt == 31, (rxTb, rWt), (rps,))
                        xs, rxs = sm_rot.next()
                        op_tt(S, "dve", xs[:, :], ps[:, 0:128], dtb[:, :], ALU.add, (rps, rcst), (rxs,))
                        ab, rab = sm_rot.next()
                        op_act(S, ab[:, :], xs[:, :], AF.Abs, (rxs,), (rab,))
                        op_act(S, ab[:, :], ab[:, :], AF.Exp, (rab,), (rab,), scale=-1.0)
                        op_act(S, ab[:, :], ab[:, :], AF.Ln, (rab,), (rab,), bias=1.0)
                        dt_, rdt = sm_rot.next()
                        op_stt(S, dt_[:, :], xs[:, :], 0.0, ab[:, :], ALU.max, ALU.add, (rxs, rab), (rdt,))
                        r0 = b * TB + tt * 128
                        op_dma(S, dr["dt"][r0:r0 + 128, :], dt_[:, :], (rdt,), ())
                        da, rda = sm_rot.next()
                        op_tt(S, "dve", da[:, :], dt_[:, :], Ab[:, :], ALU.mult, (rdt, rcst), (rda,))
                        op_dma(S, dr["dtA"][r0:r0 + 128, :], da[:, :], (rda,), ())
                col += ncol
        S.flush()


def dbg_tile(P, name, ap, res, shape, dt=F32):
    if name not in getattr(P.cfg, "dbgt", ()):
        return
    d = P.dout("dbgt_" + name, shape, dt)
    op_dma(P.S, d, ap, (res,), ())


def bc(ap2, n):
    k = ap2.shape[1]
    return ap2.unsqueeze(2).to_broadcast([128, k, n])


def ssd_phaseB(P, j):
    nc, S, cfg, c, dr = P.nc, P.S, P.cfg, P.c, P.dr
    TT = cfg.TT
    cr = c["res"]
    with ExitStack() as st:
        R = sbt(st, nc, "b_R", [128, 8, 1024], F32)
        rR = [Res() for _ in range(8)]
        asum = sbt(st, nc, "b_asum", [128, 128], F32)
        rasum = Res()
        dsk = sbt(st, nc, "b_dsk", [128, 128], F32)
        rdsk = Res()
        sm = Rot([sbt(st, nc, "b_sm%d" % i, [128, 128], F32) for i in range(24)])
        smb = Rot([sbt(st, nc, "b_smb%d" % i, [128, 128], BF16) for i in range(12)])
        dhl = Rot([sbt(st, nc, "b_dhl%d" % i, [128, 128], BF16) for i in range(4)])
        Xb_r = Rot([sbt(st, nc, "b_Xb%d" % i, [128, 16, 64], BF16) for i in range(2)])
        Xf_r = Rot([sbt(st, nc, "b_Xf%d" % i, [128, 16, 64], F32) for i in range(2)])
        Xdt_r = Rot([sbt(st, nc, "b_Xdt%d" % i, [128, 16, 64], BF16) for i in range(2)])
        Xd2_r = Rot([sbt(st, nc, "b_Xd2%d" % i, [128, 16, 64], BF16) for i in range(2)])
        tmp_r = Rot([sbt(st, nc, "b_tmp%d" % i, [128, 16, 64], F32) for i in range(2)])
        y1_r = Rot([sbt(st, nc, "b_y1%d" % i, [128, 1024], F32) for i in range(2)])
        Sk_r = Rot([sbt(st, nc, "b_Sk%d" % i, [128, 1024], F32) for i in range(2)])
        R1_r = Rot([sbt(st, nc, "b_R1%d" % i, [128, 8, 128], BF16) for i in range(2)])
        E_r = Rot([sbt(st, nc, "b_E%d" % i, [128, 128], F32) for i in range(4)])
        WT_r = Rot([sbt(st, nc, "b_WT%d" % i, [128, 128], BF16) for i in range(4)])
        psM = Rot([pst(st, nc, "b_psM%d" % i, [128, 512], F32) for i in range(2)])
        psS = Rot([pst(st, nc, "b_psS%d" % i, [128, 512], F32) for i in range(2)])
        psY = pst(st, nc, "b_psY", [128, 1024], F32)
        rpsY = Res()
        psSt = pst(st, nc, "b_psSt", [128, 1024], F32)
        rpsSt = Res()
        op_dma(S, dsk[:, :], dr["ssd_dsk%d" % j][:, :], (), (rdsk,))
        op_memset(S, "pool", asum[:, :], 0.0, (rasum,))
        for g in range(8):
            op_memset(S, "pool", R[:, g, :], 0.0, (rR[g],))
        for k in range(TT):
            r0 = k * 128
            dtA, rdtA = sm.next()
            op_dma(S, dtA[:, :], dr["dtA"][r0:r0 + 128, :], (), (rdtA,))
            dt_, rdt = sm.next()
            op_dma(S, dt_[:, :], dr["dt"][r0:r0 + 128, :], (), (rdt,))
            dh, rdh = dhl.next()
            op_copy(S, "dve", dh[:, :], dtA[:, :], (rdtA,), (rdh,))
            dl, rdl = dhl.next()
            op_tt(S, "dve", dl[:, :], dtA[:, :], dh[:, :], ALU.subtract, (rdtA, rdh), (rdl,))
            p1, rp1 = psM.next()
            op_mm(S, p1[:, 0:128], c["utri_b"][:, :], dh[:, :], True, False, (cr, rdh), (rp1,))
            op_mm(S, p1[:, 0:128], c["utri_b"][:, :], dl[:, :], False, True, (cr, rdl), (rp1,))
            op_mm(S, p1[:, 128:256], c["ones_b"][:, :], dh[:, :], True, False, (cr, rdh), (rp1,))
            op_mm(S, p1[:, 128:256], c["ones_b"][:, :], dl[:, :], False, True, (cr, rdl), (rp1,))
            acs, racs = sm.next()
            op_copy(S, "dve", acs[:, :], p1[:, 0:128], (rp1,), (racs,))
            atot, ratot = sm.next()
            op_copy(S, "dve", atot[:, :], p1[:, 128:256], (rp1,), (ratot,))
            nacs, rnacs = sm.next()
            op_ts(S, "dve", nacs[:, :], acs[:, :], -1.0, None, ALU.mult, None, (racs,), (rnacs,))
            eacs, reacs = sm.next()
            op_act(S, eacs[:, :], acs[:, :], AF.Exp, (racs,), (reacs,))
            op_dma(S, dr["eacs"][r0:r0 + 128, :], eacs[:, :], (reacs,), ())
            dec, rdec = sm.next()
            op_tt(S, "dve", dec[:, :], atot[:, :], acs[:, :], ALU.subtract, (ratot, racs), (rdec,))
            op_act(S, dec[:, :], dec[:, :], AF.Exp, (rdec,), (rdec,))
            cdec, rcdec = sm.next()
            op_act(S, cdec[:, :], atot[:, :], AF.Exp, (ratot,), (rcdec,))
            op_dma(S, dr["cdec"][k], cdec[:, :], (rcdec,), ())
            dtdec, rdtdec = sm.next()
            op_tt(S, "dve", dtdec[:, :], dt_[:, :], dec[:, :], ALU.mult, (rdt, rdec), (rdtdec,))
            op_tt(S, "pool", asum[:, :], asum[:, :], atot[:, :], ALU.add, (rasum, ratot), (rasum,))
            for g in range(8):
                gs = slice(g * 16, (g + 1) * 16)
                Xb, rXb = Xb_r.next()
                op_dma(S, Xb[:, :, :].rearrange("p h d -> p (h d)"), dr["Xb"][r0:r0 + 128, g * 1024:(g + 1) * 1024], (), (rXb,))
                Xf, rXf = Xf_r.next()
                op_dma(S, Xf[:, :, :].rearrange("p h d -> p (h d)"), dr["Xf"][r0:r0 + 128, g * 1024:(g + 1) * 1024], (), (rXf,))
                Bb, rBb = smb.next()
                op_dma(S, Bb[:, :], dr["Bb"][r0:r0 + 128, g * 128:(g + 1) * 128], (), (rBb,))
                BT, rBT = smb.next()
                op_dma(S, BT[:, :], dr["BT"][g * 128:(g + 1) * 128, r0:r0 + 128], (), (rBT,))
                CT, rCT = smb.next()
                op_dma(S, CT[:, :], dr["CT"][g * 128:(g + 1) * 128, r0:r0 + 128], (), (rCT,))
                p2, rp2 = psM.next()
                op_mm(S, p2[:, 0:128], BT[:, :], CT[:, :], True, True, (rBT, rCT), (rp2,))
                cbt, rcbt = sm.next()
                op_copy(S, "act", cbt[:, :], p2[:, 0:128], (rp2,), (rcbt,))
                Xdt, rXdt = Xdt_r.next()
                op_tt(S, "dve", Xdt[:, :, :], Xb[:, :, :], bc(dt_[:, gs], 64), ALU.mult, (rXb, rdt), (rXdt,))
                Xd2, rXd2 = Xd2_r.next()
                op_tt(S, "pool", Xd2[:, :, :], Xb[:, :, :], bc(dtdec[:, gs], 64), ALU.mult, (rXb, rdtdec), (rXd2,))
                for hq in range(4):
                    h0 = g * 16 + hq * 4
                    R1, rR1 = R1_r.next()
                    ub = c["utri_b"][:, :].unsqueeze(1).to_broadcast([128, 4, 128])
                    op_tt(S, "dve", R1[:, 0:4, :], bc(dh[:, h0:h0 + 4], 128), ub, ALU.mult, (rdh, cr), (rR1,))
                    op_tt(S, "dve", R1[:, 4:8, :], bc(dl[:, h0:h0 + 4], 128), ub, ALU.mult, (rdl, cr), (rR1,))
                    ps, rps = psS.next()
                    op_mm(S, ps[:, :], c["ones_b"][:, :], R1[:, 0:4, :].rearrange("p a b -> p (a b)"), True, False, (cr, rR1), (rps,))
                    op_mm(S, ps[:, :], c["ones_b"][:, :], R1[:, 4:8, :].rearrange("p a b -> p (a b)"), False, False, (cr, rR1), (rps,))
                    op_mm(S, ps[:, :], c["ident_b"][:, :], c["negm4_b"][:, :], False, True, (cr,), (rps,))
                    for jj in range(4):
                        h = h0 + jj
                        hl = hq * 4 + jj
                        E, rE = E_r.next()
                        op_act(S, E[:, :], ps[:, jj * 128:(jj + 1) * 128], AF.Exp, (rps, rnacs), (rE,), bias=nacs[:, h:h + 1])
                        WT, rWT = WT_r.next()
                        op_tt(S, "dve", WT[:, :], E[:, :], cbt[:, :], ALU.mult, (rE, rcbt), (rWT,))
                        if k == 0 and g == 3 and hl == 0:
                            dbg_tile(P, "Eb", E[:, :], rE, [128, 128])
                            dbg_tile(P, "WTb", WT[:, :], rWT, [128, 128], BF16)
                            dbg_tile(P, "cbtb", cbt[:, :], rcbt, [128, 128])
                            dbg_tile(P, "Xdtb", Xdt[:, :, :].rearrange("p a b -> p (a b)"), rXdt, [128, 1024], BF16)
                            dbg_tile(P, "nacsb", nacs[:, :], rnacs, [128, 128])
                        if k == 0 and g == 0:
                            dbg_tile(P, "E%d" % hl, E[:, :], rE, [128, 128])
                            dbg_tile(P, "WT%d" % hl, WT[:, :], rWT, [128, 128], BF16)
                            if hl == 0:
                                dbg_tile(P, "cbt", cbt[:, :], rcbt, [128, 128])
                                dbg_tile(P, "nacs", nacs[:, :], rnacs, [128, 128])
                                dbg_tile(P, "R1", R1[:, :, :].rearrange("p a b -> p (a b)"), rR1, [128, 1024], BF16)
                        op_mm(S, psY[:, hl * 64:(hl + 1) * 64], WT[:, :], Xdt[:, hl, :], True, True, (rWT, rXdt), (rpsY,))
                tmp, rtmp = tmp_r.next()
                op_tt(S, "pool", tmp[:, :, :], Xf[:, :, :], bc(dsk[:, gs], 64), ALU.mult, (rXf, rdsk), (rtmp,))
                y1, ry1 = y1_r.next()
                op_tt(S, "dve", y1[:, :], psY[:, :], tmp[:, :, :].rearrange("p h d -> p (h d)"), ALU.add, (rpsY, rtmp), (ry1,))
                op_dma(S, dr["y1"][r0:r0 + 128, g * 1024:(g + 1) * 1024], y1[:, :], (ry1,), ())
                X2f = Xd2[:, :, :].rearrange("p h d -> p (h d)")
                op_mm(S, psSt[:, 0:512], Bb[:, :], X2f[:, 0:512], True, True, (rBb, rXd2), (rpsSt,))
                op_mm(S, psSt[:, 512:1024], Bb[:, :], X2f[:, 512:1024], True, True, (rBb, rXd2), (rpsSt,))
                Sk, rSk = Sk_r.next()
                op_copy(S, "act", Sk[:, :], psSt[:, :], (rpsSt,), (rSk,))
                op_dma(S, dr["states"][k, g], Sk[:, :], (rSk,), ())
                Rg = R[:, g, :].rearrange("p (h d) -> p h d", h=16)
                op_tt(S, "pool", Rg, Rg, bc(cdec[:, gs], 64), ALU.mult, (rR[g], rcdec), (rR[g],))
                op_tt(S, "pool", R[:, g, :], R[:, g, :], Sk[:, :], ALU.add, (rR[g], rSk), (rR[g],))
        op_dma(S, dr["Sfin"][:, :], R[:, :, :].rearrange("p g n -> p (g n)"), tuple(rR), ())
        op_dma(S, dr["Afin"][:, :], asum[:, :], (rasum,), ())
        S.flush()
    allgather(P, dr["Sfin"][:, :], dr["Sall"][:, :], P.res["Sall"])
    allgather(P, dr["Afin"][:, :], dr["Aall"][:, :], P.res["Aall"])


def ssd_phaseC(P, j):
    nc, S, cfg, c, dr = P.nc, P.S, P.cfg, P.c, P.dr
    cr = c["res"]
    with ExitStack() as st:
        Aall = sbt(st, nc, "c_Aall", [128, NCORE, 128], F32)
        rA = Res()
        H_r = Rot([sbt(st, nc, "c_H%d" % i, [128, 16, 64], F32) for i in range(2)])
        S_r = Rot([sbt(st, nc, "c_S%d" % i, [128, 1024], F32) for i in range(3)])
        op_dma(S, Aall[:, :, :], dr["Aall"].rearrange("(r p) h -> p r h", p=128), (P.res["Aall"],), (rA,))
        op_act(S, Aall[:, :, :], Aall[:, :, :], AF.Exp, (rA,), (rA,))
        for r in range(NCORE):
            op_ts(S, "dve", Aall[:, r, :], Aall[:, r, :], c["mrank"][:, r:r + 1], c["mrank"][:, NCORE + r:NCORE + r + 1], ALU.mult, ALU.add, (rA, cr), (rA,))
        for g in range(8):
            gs = slice(g * 16, (g + 1) * 16)
            H, rH = H_r.next()
            op_memset(S, "pool", H[:, :, :], 0.0, (rH,))
            for r in range(NCORE - 1):
                Sr, rSr = S_r.next()
                op_dma(S, Sr[:, :], dr["Sall"][r * 128:(r + 1) * 128, g * 1024:(g + 1) * 1024], (P.res["Sall"],), (rSr,))
                op_tt(S, "dve", H[:, :, :], H[:, :, :], bc(Aall[:, r, gs], 64), ALU.mult, (rH, rA), (rH,))
                Hf = H[:, :, :].rearrange("p h d -> p (h d)")
                op_stt(S, Hf, Sr[:, :], c["mrank"][:, r:r + 1], Hf, ALU.mult, ALU.add, (rSr, cr, rH), (rH,))
            op_dma(S, dr["Hinit"][:, g * 1024:(g + 1) * 1024], H[:, :, :].rearrange("p h d -> p (h d)"), (rH,), ())
        S.flush()


def ssd_phaseD(P, j):
    nc, S, cfg, c, dr = P.nc, P.S, P.cfg, P.c, P.dr
    TT = cfg.TT
    cr = c["res"]
    with ExitStack() as st:
        H = sbt(st, nc, "d_H", [128, 8, 1024], F32)
        rH = [Res() for _ in range(8)]
        ngb = sbt(st, nc, "d_ngb", [128, 8192], F32)
        rng = Res()
        sm = Rot([sbt(st, nc, "d_sm%d" % i, [128, 128], F32) for i in range(6)])
        s1 = Rot([sbt(st, nc, "d_s1%d" % i, [128, 2], F32) for i in range(8)])
        CT_r = Rot([sbt(st, nc, "d_CT%d" % i, [128, 128], BF16) for i in range(2)])
        Hb_r = Rot([sbt(st, nc, "d_Hb%d" % i, [128, 1024], BF16) for i in range(2)])
        y1_r = Rot([sbt(st, nc, "d_y1%d" % i, [128, 1024], F32) for i in range(2)])
        zs_r = Rot([sbt(st, nc, "d_zs%d" % i, [128, 1024], F32) for i in range(2)])
        yo_r = Rot([sbt(st, nc, "d_yo%d" % i, [128, 16, 64], F32) for i in range(2)])
        sq_r = Rot([sbt(st, nc, "d_sq%d" % i, [128, 1024], F32) for i in range(1)])
        yn_r = Rot([sbt(st, nc, "d_yn%d" % i, [128, 1024], BF16) for i in range(2)])
        yT_r = Rot([sbt(st, nc, "d_yT%d" % i, [128, 8, 128], BF16) for i in range(2)])
        Sk_r = Rot([sbt(st, nc, "d_Sk%d" % i, [128, 1024], F32) for i in range(2)])
        psO = Rot([pst(st, nc, "d_psO%d" % i, [128, 16, 64], F32) for i in range(2)])
        psT = Rot([pst(st, nc, "d_psT%d" % i, [128, 8, 128], BF16) for i in range(2)])
        op_dma(S, ngb[:, :], dr["ssd_ng%d" % j][:, :], (), (rng,))
        for g in range(8):
            op_dma(S, H[:, g, :], dr["Hinit"][:, g * 1024:(g + 1) * 1024], (), (rH[g],))
        for k in range(TT):
            r0 = k * 128
            eacs, reacs = sm.next()
            op_dma(S, eacs[:, :], dr["eacs"][r0:r0 + 128, :], (), (reacs,))
            cdec, rcdec = sm.next()
            op_dma(S, cdec[:, :], dr["cdec"][k], (), (rcdec,))
            for g in range(8):
                gs = slice(g * 16, (g + 1) * 16)
                CT, rCT = CT_r.next()
                op_dma(S, CT[:, :], dr["CT"][g * 128:(g + 1) * 128, r0:r0 + 128], (), (rCT,))
                Hb, rHb = Hb_r.next()
                op_copy(S, "act", Hb[:, :], H[:, g, :], (rH[g],), (rHb,))
                po, rpo = psO.next()
                pof = po[:, :, :].rearrange("p h d -> p (h d)")
                op_mm(S, pof[:, 0:512], CT[:, :], Hb[:, 0:512], True, True, (rCT, rHb), (rpo,))
                op_mm(S, pof[:, 512:1024], CT[:, :], Hb[:, 512:1024], True, True, (rCT, rHb), (rpo,))
                y1, ry1 = y1_r.next()
                op_dma(S, y1[:, :], dr["y1"][r0:r0 + 128, g * 1024:(g + 1) * 1024], (), (ry1,))
                zs, rzs = zs_r.next()
                op_dma(S, zs[:, :], dr["zs"][r0:r0 + 128, g * 1024:(g + 1) * 1024], (), (rzs,))
                yo, ryo = yo_r.next()
                op_tt(S, "dve", yo[:, :, :], po[:, :, :], bc(eacs[:, gs], 64), ALU.mult, (rpo, reacs), (ryo,))
                yof = yo[:, :, :].rearrange("p h d -> p (h d)")
                op_tt(S, "pool", yof, yof, y1[:, :], ALU.add, (ryo, ry1), (ryo,))
                op_tt(S, "pool", yof, yof, zs[:, :], ALU.mult, (ryo, rzs), (ryo,))
                sq, rsq = sq_r.next()
                ss, rss = s1.next()
                op_act(S, sq[:, :], yof, AF.Square, (ryo,), (rsq, rss), accum=ss[:, 0:1])
                op_ts(S, "dve", ss[:, 0:1], ss[:, 0:1], 1.0 / 1024.0, RMS_EPS, ALU.mult, ALU.add, (rss,), (rss,))
                op_act(S, ss[:, 0:1], ss[:, 0:1], AF.Sqrt, (rss,), (rss,))
                S.add("dve", lambda e, a=ss[:, 1:2], b=ss[:, 0:1]: e.reciprocal(a, b), (rss,), (rss,))
                yn, ryn = yn_r.next()
                op_stt(S, yn[:, :], yof, ss[:, 1:2], ngb[:, g * 1024:(g + 1) * 1024], ALU.mult, ALU.mult, (ryo, rss, rng), (ryn,))
                pt, rpt = psT.next()
                for i in range(8):
                    op_tr(S, pt[:, i, :], yn[:, i * 128:(i + 1) * 128], c["ident_b"][:, :], (ryn, cr), (rpt,))
                yT, ryT = yT_r.next()
                op_copy(S, "act", yT[:, :, :], pt[:, :, :], (rpt,), (ryT,))
                op_dma(S, dr["ynT"][g * 1024:(g + 1) * 1024, r0:r0 + 128].rearrange("(i p) t -> p i t", p=128), yT[:, :, :], (ryT,), ())
                Sk, rSk = Sk_r.next()
                op_dma(S, Sk[:, :], dr["states"][k, g], (), (rSk,))
                Hg = H[:, g, :].rearrange("p (h d) -> p h d", h=16)
                op_tt(S, "pool", Hg, Hg, bc(cdec[:, gs], 64), ALU.mult, (rH[g], rcdec), (rH[g],))
                op_tt(S, "pool", H[:, g, :], H[:, g, :], Sk[:, :], ALU.add, (rH[g], rSk), (rH[g],))
        S.flush()


def phase_outproj(P, yT, C, W, rW, li, xsrc, last):
    nc, S, cfg, c, dr = P.nc, P.S, P.cfg, P.c, P.dr
    TT = cfg.TT
    CT_ = C // 128
    Wd = W.rearrange("(ct p) n -> p ct n", p=128)
    yTd = yT.rearrange("(ct p) t -> p ct t", p=128)
    with ExitStack() as st:
        Wc = sbt(st, nc, "e_Wc", [128, CT_, 512], BF16)
        rWc = Res()
        y_r = Rot([sbt(st, nc, "e_y%d" % i, [128, CT_, 128], BF16) for i in range(2)])
        o_r = Rot([sbt(st, nc, "e_o%d" % i, [128, 512], F32) for i in range(2)])
        ps_r = Rot([pst(st, nc, "e_ps%d" % i, [128, 512], F32) for i in range(2)])
        for n in range(8):
            op_dma(S, Wc[:, :, :], Wd[:, :, n * 512:(n + 1) * 512], (rW,), (rWc,))
            for tt in range(TT):
                y, ry = y_r.next()
                op_dma(S, y[:, :, :], yTd[:, :, tt * 128:(tt + 1) * 128], (), (ry,))
                ps, rps = ps_r.next()
                for ct in range(CT_):
                    op_mm(S, ps[:, :], y[:, ct, :], Wc[:, ct, :], ct == 0, ct == CT_ - 1, (ry, rWc), (rps,))
                o, ro = o_r.next()
                op_copy(S, "act", o[:, :], ps[:, :], (rps,), (ro,))
                op_dma(S, dr["pre"][tt * 128:(tt + 1) * 128, n * 512:(n + 1) * 512], o[:, :], (ro,), ())
        S.flush()
    with ExitStack() as st:
        lng = sbt(st, nc, "f_lng", [128, D], F32)
        lnb = sbt(st, nc, "f_lnb", [128, D], F32)
        rln = Res()
        x_r = Rot([sbt(st, nc, "f_x%d" % i, [128, D], F32) for i in range(2)])
        p_r = Rot([sbt(st, nc, "f_p%d" % i, [128, D], F32) for i in range(2)])
        st_r = Rot([sbt(st, nc, "f_st%d" % i, [128, 8, 6], F32) for i in range(2)])
        mv_r = Rot([sbt(st, nc, "f_mv%d" % i, [128, 4], F32) for i in range(2)])
        bufs = (Rot([sbt(st, nc, "f_xb%d" % i, [128, D], BF16) for i in range(2)]),
                Rot([pst(st, nc, "f_ps%d" % i, [128, 8, 128], BF16) for i in range(4)]),
                Rot([sbt(st, nc, "f_xTt%d" % i, [128, 32, 128], BF16) for i in range(2)]))
        op_dma(S, lng[:, :], dr["ln_g%d" % li][:, :], (), (rln,))
        op_dma(S, lnb[:, :], dr["ln_b%d" % li][:, :], (), (rln,))
        dst = dr["out"] if last else dr["xres"]
        for tt in range(TT):
            rows = slice(tt * 128, (tt + 1) * 128)
            x, rx = x_r.next()
            op_dma(S, x[:, :], xsrc[rows, :], (), (rx,))
            p, rp = p_r.next()
            op_dma(S, p[:, :], dr["pre"][rows, :], (), (rp,))
            op_stt(S, p[:, :], x[:, :], ALPHA, p[:, :], ALU.mult, ALU.add, (rx, rp), (rp,))
            sts, rst = st_r.next()
            for i in range(8):
                S.add("dve", lambda e, a=sts[:, i, :], b=p[:, i * 512:(i + 1) * 512]: e.bn_stats(a, b), (rp,), (rst,))
            mv, rmv = mv_r.next()
            S.add("dve", lambda e, a=mv[:, 0:2], b=sts[:, :, :].rearrange("p a b -> p (a b)"): e.bn_aggr(a, b), (rst,), (rmv,))
            op_ts(S, "dve", mv[:, 2:3], mv[:, 1:2], LN_EPS, None, ALU.add, None, (rmv,), (rmv,))
            op_act(S, mv[:, 2:3], mv[:, 2:3], AF.Sqrt, (rmv,), (rmv,))
            S.add("dve", lambda e, a=mv[:, 3:4], b=mv[:, 2:3]: e.reciprocal(a, b), (rmv,), (rmv,))
            op_ts(S, "dve", p[:, :], p[:, :], mv[:, 0:1], mv[:, 3:4], ALU.subtract, ALU.mult, (rp, rmv), (rp,))
            op_tt(S, "pool", p[:, :], p[:, :], lng[:, :], ALU.mult, (rp, rln), (rp,))
            op_tt(S, "pool", p[:, :], p[:, :], lnb[:, :], ALU.add, (rp, rln), (rp,))
            op_dma(S, dst[rows, :], p[:, :], (rp,), ())
            if not last:
                emit_xT(P, bufs, p, rp, tt)
        S.flush()
    if not last:
        allgather(P, dr["tail"][:, :], dr["tails"][:, :], P.res["tails"])


QOFF, KOFF, VOFF, ZOFF, QIOFF, KIOFF, WIOFF = 0, 4096, 5120, 6144, 10240, 18432, 18560
WSCALE = (64 ** -0.5) * (128 ** -0.5)
ASCALE = 128 ** -0.5
LO0 = -64.0
NITER = 31


def dsa_decls(P):
    cfg = P.cfg
    T, TT, L = cfg.T, cfg.TT, cfg.L
    P.dtmp("qT", [4096, T], BF16)
    P.dtmp("kT", [1024, T], BF16)
    P.dtmp("kTall", [NCORE * 1024, T], BF16)
    P.dtmp("V", [T, 1024], BF16)
    P.dtmp("Vall", [NCORE * T, 1024], BF16)
    P.dtmp("zT", [4096, T])
    P.dtmp("qiT", [8192, T], BF16)
    P.dtmp("kiT", [128, T], BF16)
    P.dtmp("kiTall", [NCORE * 128, T], BF16)
    P.dtmp("wT2", [128, T], BF16)
    P.dtmp("ogT", [4096, T], BF16)
    for nm in ("kTall", "Vall", "kiTall"):
        P.res[nm] = Res()
    P.din("c_cosC", [32, T])
    P.din("c_sinC", [32, T])
    P.din("c_cosK", [128, TT, 16])
    P.din("c_sinK", [128, TT, 16])
    P.din("c_brow", [2, TT, L])
    P.din("c_mrow", [2, 128])
    ndsa = sum(1 for l in cfg.layers if l == 1)
    for j in range(ndsa):
        P.din("dsa_kng%d" % j, [128, 128])
        P.din("dsa_knb%d" % j, [128, 128])


def dsa_phaseA(P, j, W, kv_only=False):
    nc, S, cfg, c, dr = P.nc, P.S, P.cfg, P.c, P.dr
    T, TB = cfg.T, cfg.TB
    NB, NTB = T // TB, TB // 128
    NCH = (TB + 511) // 512
    CW = TB // NCH
    cr = c["res"]
    Wd = W.rearrange("(kt p) n -> p kt n", p=128)
    xTd = dr["xT"].rearrange("(kt p) t -> p kt t", p=128)
    rW = P.res["dsa_in_w%d" % j]
    with ExitStack() as st:
        xTb = sbt(st, nc, "g_xTb", [128, 32, TB], BF16)
        rxTb = Res()
        Wg = Rot([sbt(st, nc, "g_Wg%d" % i, [128, 32, 512], BF16) for i in range(2)])
        cosC = sbt(st, nc, "g_cosC", [32, TB], F32)
        sinC = sbt(st, nc, "g_sinC", [32, TB], F32)
        rcs = Res()
        cosK = sbt(st, nc, "g_cosK", [128, cfg.TT, 16], F32)
        sinK = sbt(st, nc, "g_sinK", [128, cfg.TT, 16], F32)
        kng = sbt(st, nc, "g_kng", [128, 128], F32)
        knb = sbt(st, nc, "g_knb", [128, 128], F32)
        rcst = Res()
        u_rot = Rot([sbt(st, nc, "g_u%d" % i, [128, TB], F32) for i in range(2)])
        r_rot = Rot([sbt(st, nc, "g_r%d" % i, [32, TB], F32) for i in range(2)])
        t1_rot = Rot([sbt(st, nc, "g_t1%d" % i, [32, TB], F32) for i in range(2)])
        t2_rot = Rot([sbt(st, nc, "g_t2%d" % i, [32, TB], F32) for i in range(2)])
        qb_rot = Rot([sbt(st, nc, "g_qb%d" % i, [128, TB], BF16) for i in range(2)])
        vt_rot = Rot([sbt(st, nc, "g_vt%d" % i, [128, 512], BF16) for i in range(2)])
        sm = Rot([sbt(st, nc, "g_sm%d" % i, [128, 128], F32) for i in range(4)])
        smb = Rot([sbt(st, nc, "g_smb%d" % i, [128, 128], BF16) for i in range(4)])
        st_r = Rot([sbt(st, nc, "g_st%d" % i, [128, 8], F32) for i in range(2)])
        psA = Rot([pst(st, nc, "g_psA%d" % i, [128, 512], F32) for i in range(2)])
        psU = Rot([pst(st, nc, "g_psU%d" % i, [128, NCH, 512], F32) for i in range(2)])
        psB = Rot([pst(st, nc, "g_psB%d" % i, [128, 8, 128], BF16) for i in range(1)])
        op_dma(S, cosK[:, :, :], dr["c_cosK"][:, :, :], (), (rcst,))
        op_dma(S, sinK[:, :, :], dr["c_sinK"][:, :, :], (), (rcst,))
        op_dma(S, kng[:, :], dr["dsa_kng%d" % j][:, :], (), (rcst,))
        op_dma(S, knb[:, :], dr["dsa_knb%d" % j][:, :], (), (rcst,))
        for b in range(NB):
            t0 = b * TB
            op_dma(S, xTb[:, :, :], xTd[:, :, t0:t0 + TB], (), (rxTb,))
            op_dma(S, cosC[:, :], dr["c_cosC"][:, t0:t0 + TB], (), (rcs,))
            op_dma(S, sinC[:, :], dr["c_sinC"][:, t0:t0 + TB], (), (rcs,))
            col = 0
            while col < DSA_IN:
                ncol = min(512, DSA_IN - col)
                if col == KIOFF:
                    ncol = 128
                if kv_only and not (KOFF <= col < ZOFF or col == KIOFF):
                    col += 64 if col == WIOFF else ncol
                    continue
                wc = col
                if kv_only:
                    wc = col - KOFF if col < ZOFF else 2048
                Wt, rWt = Wg.next()
                if col == WIOFF:
                    op_dma(S, Wt[:, :, 0:64], Wd[:, :, wc:wc + 64], (rW,), (rWt,))
                    op_dma(S, Wt[:, :, 64:128], Wd[:, :, wc:wc + 64], (rW,), (rWt,))
                    ncol = 64
                else:
                    op_dma(S, Wt[:, :, :ncol], Wd[:, :, wc:wc + ncol], (rW,), (rWt,))
                if VOFF <= col < ZOFF:
                    for tt in range(NTB):
                        ps, rps = psA.next()
                        for kt in range(32):
                            op_mm(S, ps[:, :], xTb[:, kt, tt * 128:(tt + 1) * 128], Wt[:, kt, :], kt == 0, kt == 31, (rxTb, rWt), (rps,))
                        vt, rvt = vt_rot.next()
                        op_copy(S, "act", vt[:, :], ps[:, :], (rps,), (rvt,))
                        op_dma(S, dr["V"][t0 + tt * 128:t0 + (tt + 1) * 128, col - VOFF:col - VOFF + 512], vt[:, :], (rvt,), ())
                elif col == KIOFF:
                    for tt in range(NTB):
                        gt = b * NTB + tt
                        ps, rps = psA.next()
                        for kt in range(32):
                            op_mm(S, ps[:, 0:128], xTb[:, kt, tt * 128:(tt + 1) * 128], Wt[:, kt, 0:128], kt == 0, kt == 31, (rxTb, rWt), (rps,))
                        x, rx = sm.next()
                        op_copy(S, "act", x[:, :], ps[:, 0:128], (rps,), (rx,))
                        sts, rst = st_r.next()
                        S.add("dve", lambda e, a=sts[:, 0:6], bb=x[:, :]: e.bn_stats(a, bb), (rx,), (rst,))
                        mv, rmv = st_r.next()
                        S.add("dve", lambda e, a=mv[:, 0:2], bb=sts[:, 0:6]: e.bn_aggr(a, bb), (rst,), (rmv,))
                        op_ts(S, "dve", mv[:, 2:3], mv[:, 1:2], LN_EPS, None, ALU.add, None, (rmv,), (rmv,))
                        op_act(S, mv[:, 2:3], mv[:, 2:3], AF.Sqrt, (rmv,), (rmv,))
                        S.add("dve", lambda e, a=mv[:, 3:4], bb=mv[:, 2:3]: e.reciprocal(a, bb), (rmv,), (rmv,))
                        op_ts(S, "dve", x[:, :], x[:, :], mv[:, 0:1], mv[:, 3:4], ALU.subtract, ALU.mult, (rx, rmv), (rx,))
                        op_tt(S, "dve", x[:, :], x[:, :], kng[:, :], ALU.mult, (rx, rcst), (rx,))
                        op_tt(S, "dve", x[:, :], x[:, :], knb[:, :], ALU.add, (rx, rcst), (rx,))
                        y, ry = sm.next()
                        op_copy(S, "act", y[:, :], x[:, :], (rx,), (ry,))
                        t1, rt1 = sm.next()
                        op_tt(S, "dve", t1[:, 0:16], x[:, 0:16], cosK[:, gt, :], ALU.mult, (rx, rcst), (rt1,))
                        op_tt(S, "dve", t1[:, 16:32], x[:, 16:32], sinK[:, gt, :], ALU.mult, (rx, rcst), (rt1,))
                        op_tt(S, "dve", y[:, 0:16], t1[:, 0:16], t1[:, 16:32], ALU.subtract, (rt1, ry), (ry,))
                        op_tt(S, "dve", t1[:, 32:48], x[:, 16:32], cosK[:, gt, :], ALU.mult, (rx, rcst), (rt1,))
                        op_tt(S, "dve", t1[:, 48:64], x[:, 0:16], sinK[:, gt, :], ALU.mult, (rx, rcst), (rt1,))
                        op_tt(S, "dve", y[:, 16:32], t1[:, 32:48], t1[:, 48:64], ALU.add, (rt1, ry), (ry,))
                        yb, ryb = smb.next()
                        op_copy(S, "act", yb[:, :], y[:, :], (ry,), (ryb,))
                        pb, rpb = psB.next()
                        op_tr(S, pb[:, 0, :], yb[:, :], c["ident_b"][:, :], (ryb, cr), (rpb,))
                        kt_, rkt = smb.next()
                        op_copy(S, "dve", kt_[:, :], pb[:, 0, :], (rpb,), (rkt,))
                        op_dma(S, dr["kiT"][:, t0 + tt * 128:t0 + (tt + 1) * 128], kt_[:, :], (rkt,), ())
                else:
                    nsub = 1 if col == WIOFF else ncol // 128
                    for s_ in range(nsub):
                        pu, rpu = psU.next()
                        for ch in range(NCH):
                            for kt in range(32):
                                op_mm(S, pu[:, ch, :CW], Wt[:, kt, s_ * 128:(s_ + 1) * 128], xTb[:, kt, ch * CW:(ch + 1) * CW],
                                      kt == 0, kt == 31, (rxTb, rWt), (rpu,))
                        puv = pu[:, :, :CW]
                        if ZOFF <= col < QIOFF:
                            u, ru = u_rot.next()
                            op_act(S, u[:, :].rearrange("p (c w) -> p c w", c=NCH), puv, AF.Silu, (rpu,), (ru,))
                            row = col - ZOFF + s_ * 128
                            op_dma(S, dr["zT"][row:row + 128, t0:t0 + TB], u[:, :], (ru,), ())
                        elif col == WIOFF:
                            qb, rqb = qb_rot.next()
                            S.add("act", lambda e, a=qb[:, :].rearrange("p (c w) -> p c w", c=NCH), bb=puv: e.mul(a, bb, WSCALE), (rpu,), (rqb,))
                            op_dma(S, dr["wT2"][:, t0:t0 + TB], qb[:, :], (rqb,), ())
                        else:
                            u, ru = u_rot.next()
                            op_copy(S, "act", u[:, :].rearrange("p (c w) -> p c w", c=NCH), puv, (rpu,), (ru,))
                            r, rr = r_rot.next()
                            op_dma(S, r[0:16, :], u[16:32, :], (ru,), (rr,))
                            op_dma(S, r[16:32, :], u[0:16, :], (ru,), (rr,))
                            t1, rt1 = t1_rot.next()
                            op_tt(S, "dve", t1[:, :], u[0:32, :], cosC[:, :], ALU.mult, (ru, rcs), (rt1,))
                            t2, rt2 = t2_rot.next()
                            op_tt(S, "pool", t2[:, :], r[:, :], sinC[:, :], ALU.mult, (rr, rcs), (rt2,))
                            qb, rqb = qb_rot.next()
                            op_copy(S, "act", qb[:, :], u[:, :], (ru,), (rqb,))
                            op_tt(S, "dve", qb[0:32, :], t1[:, :], t2[:, :], ALU.add, (rt1, rt2, rqb), (rqb,))
                            if col < KOFF:
                                dst, row = dr["qT"], col - QOFF + s_ * 128
                            elif col < VOFF:
                                dst, row = dr["kT"], col - KOFF + s_ * 128
                            else:
                                dst, row = dr["qiT"], col - QIOFF + s_ * 128
                            op_dma(S, dst[row:row + 128, t0:t0 + TB], qb[:, :], (rqb,), ())
                col += ncol
        S.flush()
    allgather(P, dr["kT"][:, :], dr["kTall"][:, :], P.res["kTall"])
    allgather(P, dr["V"][:, :], dr["Vall"][:, :], P.res["Vall"])
    allgather(P, dr["kiT"][:, :], dr["kiTall"][:, :], P.res["kiTall"])


def dsa_phaseB(P, j):
    nc, S, cfg, c, dr = P.nc, P.S, P.cfg, P.c, P.dr
    T, TT, KC = cfg.T, cfg.TT, cfg.KC
    cr = c["res"]
    base = (NCORE - 1) * TT if (cfg.coll or getattr(cfg, "multi", False)) else 0
    SMAX = (base + TT) * 128
    NTMAX = base + TT
    KVC = min(2048, T)
    with ExitStack() as st:
        scores = sbt(st, nc, "h_scores", [128, SMAX], F32)
        rsc = Res()
        sel = sbt(st, nc, "h_sel", [128, SMAX], BF16)
        rsel = Res()
        selT = scores[:, 0:SMAX // 2].bitcast(BF16).rearrange("p (t q) -> p t q", q=128)
        rselT = rsc
        arena = sbt(st, nc, "h_arena", [128, 8192], BF16)
        rQ = Res()
        stg = sbt(st, nc, "h_stg", [128, 8192], BF16)
        rStg = Res()
        LW = sbt(st, nc, "h_LW", [128, 64, 128], BF16)
        rLW = Res()
        wt2 = sbt(st, nc, "h_wt2", [128, 128], BF16)
        rwt2 = Res()
        mrow = sbt(st, nc, "h_mrow", [2, 128], BF16)
        mrowf = sbt(st, nc, "h_mrowf", [2, 128], F32)
        rmrow = Res()
        ki_r = Rot([sbt(st, nc, "h_ki%d" % i, [128, KC], BF16) for i in range(2)])
        br_r = Rot([sbt(st, nc, "h_br%d" % i, [2, KC], BF16) for i in range(2)])
        brf_r = Rot([sbt(st, nc, "h_brf%d" % i, [2, KC], F32) for i in range(2)])
        R_r = Rot([sbt(st, nc, "h_R%d" % i, [128, KC], BF16) for i in range(4)])
        v1 = Rot([sbt(st, nc, "h_v1%d" % i, [128, 4], F32) for i in range(4)])
        lo = sbt(st, nc, "h_lo", [128, 1], F32)
        rlo = Res()
        QT_r = Rot([sbt(st, nc, "h_QT%d" % i, [128, 4, 128], BF16) for i in range(2)])
        P_r = Rot([sbt(st, nc, "h_P%d" % i, [128, 4, 128], BF16) for i in range(2)])
        Pm_r = Rot([sbt(st, nc, "h_Pm%d" % i, [128, 4, 128], BF16) for i in range(2)])
        og_r = Rot([sbt(st, nc, "h_og%d" % i, [128, 512], F32) for i in range(2)])
        rd_r = Rot([sbt(st, nc, "h_rd%d" % i, [128, 512], F32) for i in range(1)])
        z_r = Rot([sbt(st, nc, "h_z%d" % i, [128, 4, 128], F32) for i in range(1)])
        ob_r = Rot([sbt(st, nc, "h_ob%d" % i, [128, 4, 128], BF16) for i in range(2)])
        psG = [(pst(st, nc, "h_psG%d" % i, [128, 512], F32), Res()) for i in range(7)]
        psTb = (pst(st, nc, "h_psT", [128, 8, 128], BF16), Res())
        ps_score = Rot.__new__(Rot); ps_score.t = psG[0:2]; ps_score.i = 0
        ps_rel = Rot.__new__(Rot); ps_rel.t = psG[2:6]; ps_rel.i = 0
        ps_L = Rot.__new__(Rot); ps_L.t = psG[2:4]; ps_L.i = 0
        psO, rpsO = psG[4]
        psD, rpsD = psG[5]
        QIT = stg[:, :].rearrange("p (h q) -> p h q", h=64)
        QITq = arena[:, :].rearrange("p (q h) -> p q h", h=64)
        kv_bufs = [(stg[:, i * 2048:(i + 1) * 2048], rStg) for i in range(4)]
        op_memset(S, "pool", LW[:, :, :], 0.0, (rLW,))
        op_dma(S, mrowf[:, :], dr["c_mrow"][:, :], (), (rmrow,))
        op_copy(S, "dve", mrow[:, :], mrowf[:, :], (rmrow,), (rmrow,))
        qiTd = dr["qiT"].rearrange("(h d) t -> d h t", d=128)
        LWf = LW[:, :, :].rearrange("p a b -> p (a b)")
        for qt in range(TT):
            NT = base + qt + 1
            SJ = NT * 128
            qs = slice(qt * 128, (qt + 1) * 128)
            op_dma(S, QIT, qiTd[:, :, qs], (), (rStg,))
            op_copy(S, "pool", QITq, stg[:, :].rearrange("p (h q) -> p q h", h=64), (rStg,), (rQ,))
            op_dma(S, wt2[:, :], dr["wT2"][:, qs], (), (rwt2,))
            op_copy(S, "dve", LWf[0:64, 0:63 * 130 + 1:130], wt2[0:64, 0:128:2], (rwt2, rLW), (rLW,))
            op_copy(S, "dve", LWf[64:128, 1:63 * 130 + 2:130], wt2[64:128, 1:128:2], (rwt2, rLW), (rLW,))
            s0 = 0
            while s0 < SJ:
                N = min(KC, SJ - s0)
                rk, colk = s0 // T, s0 % T
                ki, rki = ki_r.next()
                op_dma(S, ki[:, :N], dr["kiTall"][rk * 128:(rk + 1) * 128, colk:colk + N], (P.res["kiTall"],), (rki,))
                brf, rbrf = brf_r.next()
                op_dma(S, brf[:, :N], dr["c_brow"][:, qt, s0:s0 + N], (), (rbrf,))
                br, rbr = br_r.next()
                op_copy(S, "pool", br[:, :N], brf[:, :N], (rbrf,), (rbr,))
                psc, rpsc = ps_score.next()
                op_mm(S, psc[:, :N], mrow[:, :], br[:, :N], True, False, (rmrow, rbr), (rpsc,))
                for jj in range(64):
                    pr, rpr = ps_rel.next()
                    lhs = QITq[:, 2 * jj:2 * jj + 2, :].rearrange("d q h -> d (q h)")
                    op_mm(S, pr[:, :N], lhs, ki[:, :N], True, True, (rQ, rki), (rpr,))
                    R, rR = R_r.next()
                    if jj % 2 == 0:
                        op_act(S, R[:, :N], pr[:, :N], AF.Relu, (rpr,), (rR,))
                    else:
                        op_ts(S, "dve", R[:, :N], pr[:, :N], 0.0, None, ALU.max, None, (rpr,), (rR,))
                    op_mm(S, psc[:, :N], LW[:, jj, :], R[:, :N], False, jj == 63, (rLW, rR), (rpsc,))
                op_copy(S, "act", scores[:, s0:s0 + N], psc[:, :N], (rpsc,), (rsc,))
                s0 += N
            op_memset(S, "dve", lo[:, :], LO0, (rlo,))
            for it in range(NITER):
                w = -LO0 * (2.0 ** -it)
                vv, rvv = v1.next()
                op_ts(S, "dve", vv[:, 0:1], lo[:, :], w, None, ALU.add, None, (rlo,), (rvv,))
                op_ts(S, "dve", sel[:, :SJ], scores[:, :SJ], vv[:, 0:1], 0.0, ALU.is_ge, ALU.add, (rsc, rvv), (rsel, rvv), accum=vv[:, 1:2])
                op_ts(S, "dve", vv[:, 2:3], vv[:, 1:2], float(TOPK), w, ALU.is_ge, ALU.mult, (rvv,), (rvv,))
                op_tt(S, "dve", lo[:, :], lo[:, :], vv[:, 2:3], ALU.add, (rlo, rvv), (rlo,))
            op_ts(S, "dve", sel[:, :SJ], scores[:, :SJ], lo[:, 0:1], None, ALU.is_ge, None, (rsc, rlo), (rsel,))
            for t8 in range(0, NT, 8):
                n8 = min(8, NT - t8)
                pt, rpt = psTb
                for i in range(n8):
                    op_tr(S, pt[:, i, :], sel[:, (t8 + i) * 128:(t8 + i + 1) * 128], c["ident_b"][:, :], (rsel, cr), (rpt,))
                op_copy(S, "act", selT[:, t8:t8 + n8, :], pt[:, 0:n8, :], (rpt,), (rselT,))
            for hk in range(8):
                QT, rQT = QT_r.next()
                op_dma(S, QT[:, :, :], dr["qT"][hk * 512:(hk + 1) * 512, qs].rearrange("(i d) t -> d i t", d=128), (), (rQT,))
                QTf = QT[:, :, :].rearrange("p a b -> p (a b)")
                nchunk = (SJ + KVC - 1) // KVC
                for kc in range(nchunk):
                    k0 = kc * KVC
                    kn = min(KVC, SJ - k0)
                    rk, colk = k0 // T, k0 % T
                    Kb, rKb = kv_bufs[0]
                    Vb, rVb = kv_bufs[1]
                    op_dma(S, Kb[:, :kn], dr["kTall"][rk * 1024 + hk * 128:rk * 1024 + (hk + 1) * 128, colk:colk + kn], (P.res["kTall"],), (rKb,))
                    Vb3 = Vb.rearrange("p (t d) -> p t d", d=128)
                    op_dma(S, Vb3[:, :kn // 128, :], dr["Vall"][k0:k0 + kn, hk * 128:(hk + 1) * 128].rearrange("(t p) d -> p t d", p=128),
                           (P.res["Vall"],), (rVb,))
                    for tl in range(kn // 128):
                        t = k0 // 128 + tl
                        pl, rpl = ps_L.next()
                        op_mm(S, pl[:, :], Kb[:, tl * 128:(tl + 1) * 128], QTf, True, True, (rKb, rQT), (rpl,))
                        Pt, rPt = P_r.next()
                        op_act(S, Pt[:, :, :].rearrange("p a b -> p (a b)"), pl[:, :], AF.Exp, (rpl,), (rPt,), scale=ASCALE)
                        Pm, rPm = Pm_r.next()
                        op_tt(S, "dve", Pm[:, :, :], Pt[:, :, :], selT[:, t, :].unsqueeze(1).to_broadcast([128, 4, 128]), ALU.mult, (rPt, rselT), (rPm,))
                        Pmf = Pm[:, :, :].rearrange("p a b -> p (a b)")
                        op_mm(S, psO[:, :], Vb3[:, tl, :], Pmf, t == 0, t == NT - 1, (rVb, rPm), (rpsO,))
                        op_mm(S, psD[:, :], c["ones_b"][:, :], Pmf, t == 0, t == NT - 1, (cr, rPm), (rpsD,))
                rd, rrd = rd_r.next()
                S.add("dve", lambda e, a=rd[:, :], bb=psD[:, :]: e.reciprocal(a, bb), (rpsD,), (rrd,))
                og, rog = og_r.next()
                op_tt(S, "dve", og[:, :], psO[:, :], rd[:, :], ALU.mult, (rpsO, rrd), (rog,))
                z, rz = z_r.next()
                op_dma(S, z[:, :, :], dr["zT"][hk * 512:(hk + 1) * 512, qs].rearrange("(i d) t -> d i t", d=128), (), (rz,))
                ob, rob = ob_r.next()
                op_tt(S, "pool", ob[:, :, :].rearrange("p a b -> p (a b)"), og[:, :], z[:, :, :].rearrange("p a b -> p (a b)"), ALU.mult, (rog, rz), (rob,))
                op_dma(S, dr["ogT"][hk * 512:(hk + 1) * 512, qs].rearrange("(i d) t -> d i t", d=128), ob[:, :, :], (rob,), ())
        S.flush()


def dsa_layer(P, j, W):
    dsa_phaseA(P, j, W)
    if getattr(P.cfg, "stop", 99) <= 12:
        return
    dsa_phaseB(P, j)


def dsa_host(cfg, inp, core, jd, m):
    T, TT, L = cfg.T, cfg.TT, cfg.L
    if cfg.coll:
        ks = D // NCORE
        m["dsa_in_w%d" % jd] = np.ascontiguousarray(inp["dsa_in_w"][jd, core * ks:(core + 1) * ks, :])
        m["dsa_out_w%d" % jd] = np.ascontiguousarray(inp["dsa_out_w"][jd, core * ks:(core + 1) * ks, :])
    else:
        m["dsa_in_w%d" % jd] = np.ascontiguousarray(inp["dsa_in_w"][jd])
        m["dsa_out_w%d" % jd] = np.ascontiguousarray(inp["dsa_out_w"][jd])
    m["dsa_kng%d" % jd] = bcast128(inp["dsa_kn_g"][jd])
    m["dsa_knb%d" % jd] = bcast128(inp["dsa_kn_b"][jd])
    if "c_cosC" in m:
        return
    dsa_tables(cfg, core, m)


def dsa_tables(cfg, core, m):
    T, TT, L = cfg.T, cfg.TT, cfg.L
    pos = (core * T + np.arange(T)).astype(np.float32)
    inv = np.power(np.float32(ROPE_THETA), (-2.0 * np.arange(16, dtype=np.float32) / 32.0).astype(np.float32)).astype(np.float32)
    ang = (pos[:, None] * inv[None, :]).astype(np.float32)
    cs, sn = np.cos(ang).astype(np.float32), np.sin(ang).astype(np.float32)
    m["c_cosC"] = np.ascontiguousarray(np.concatenate([cs.T, cs.T], 0))
    m["c_sinC"] = np.ascontiguousarray(np.concatenate([-sn.T, sn.T], 0))
    m["c_cosK"] = np.ascontiguousarray(cs.reshape(TT, 128, 16).transpose(1, 0, 2))
    m["c_sinK"] = np.ascontiguousarray(sn.reshape(TT, 128, 16).transpose(1, 0, 2))
    brow = np.zeros((2, TT, L), np.float32)
    s = np.arange(L)
    for qt in range(TT):
        g = core * TT + qt
        brow[0, qt, s >= (g + 1) * 128] = NEG
        brow[1, qt, (s >= g * 128 + 64) & (s < (g + 1) * 128)] = NEG
    m["c_brow"] = brow
    mrow = np.zeros((2, 128), np.float32)
    mrow[0, :] = 1.0
    mrow[1, :64] = 1.0
    m["c_mrow"] = mrow


def build_program(cfg):
    P = Prog(cfg)
    T, TT = cfg.T, cfg.TT
    P.din("x", [T, D])
    P.dout("out", [T, D])
    P.dtmp("xres", [T, D])
    P.dtmp("xT", [D, T], BF16)
    P.dtmp("tail", [3, D], BF16)
    P.dtmp("tails", [24, D], BF16)
    P.dtmp("pre", [T, D])
    P.res["tails"] = Res()
    nssd = sum(1 for l in cfg.layers if l == 0)
    ndsa = sum(1 for l in cfg.layers if l == 1)
    Wssd_in, Wssd_out, Wdsa_in, Wdsa_out = [], [], [], []
    setup_consts(P)
    for j in range(nssd):
        Wssd_in.append(gather_weight(P, "ssd_in_w%d" % j, D, SSD_IN))
        Wssd_out.append(gather_weight(P, "ssd_out_w%d" % j, DI, D))
        P.din("ssd_cw%d" % j, [128, 80, 4])
        P.din("ssd_cb%d" % j, [128, 80])
        for nm in ("dtb", "alog", "dsk"):
            P.din("ssd_%s%d" % (nm, j), [128, 128])
        P.din("ssd_ng%d" % j, [128, DI])
    for j in range(ndsa):
        Wdsa_in.append(gather_weight(P, "dsa_in_w%d" % j, D, DSA_IN))
        Wdsa_out.append(gather_weight(P, "dsa_out_w%d" % j, D, D))
    for li in range(len(cfg.layers)):
        P.din("ln_g%d" % li, [128, D])
        P.din("ln_b%d" % li, [128, D])
    if nssd:
        P.dtmp("zs", [T, DI])
        P.dtmp("Xf", [T, DI])
        P.dtmp("Xb", [T, DI], BF16)
        P.dtmp("Bb", [T, 1024], BF16)
        P.dtmp("BT", [1024, T], BF16)
        P.dtmp("CT", [1024, T], BF16)
        P.dtmp("dt", [T, 128])
        P.dtmp("dtA", [T, 128])
        P.dtmp("eacs", [T, 128])
        P.dtmp("cdec", [TT, 128, 128])
        P.dtmp("states", [TT, 8, 128, 1024])
        P.dtmp("y1", [T, DI])
        P.dtmp("Sfin", [128, DI])
        P.dtmp("Sall", [NCORE * 128, DI])
        P.dtmp("Afin", [128, 128])
        P.dtmp("Aall", [NCORE * 128, 128])
        P.dtmp("Hinit", [128, DI])
        P.dtmp("ynT", [DI, T], BF16)
        P.res["Sall"] = Res()
        P.res["Aall"] = Res()
    if ndsa:
        dsa_decls(P)
    stop = getattr(cfg, "stop", 99)
    phase_weights(P)
    phase_prologue(P)
    if stop <= 1:
        P.S.flush()
        return P
    js = jd = 0
    for li, kind in enumerate(cfg.layers):
        last = li == len(cfg.layers) - 1
        xsrc = P.dr["x"] if li == 0 else P.dr["xres"]
        if kind == 0:
            ssd_phaseA(P, js, Wssd_in[js])
            if stop <= 2:
                add_dumps(P)
                return P
            ssd_phaseB(P, js)
            if stop <= 3:
                add_dumps(P)
                return P
            ssd_phaseC(P, js)
            if stop <= 4:
                add_dumps(P)
                return P
            ssd_phaseD(P, js)
            if stop <= 5:
                add_dumps(P)
                return P
            phase_outproj(P, P.dr["ynT"], DI, Wssd_out[js], P.res["ssd_out_w%d" % js], li, xsrc, last)
            js += 1
        else:
            dsa_layer(P, jd, Wdsa_in[jd])
            phase_outproj(P, P.dr["ogT"], D, Wdsa_out[jd], P.res["dsa_out_w%d" % jd], li, xsrc, last)
            jd += 1
    add_dumps(P)
    return P


def add_dumps(P):
    for name in getattr(P.cfg, "dump", ()):
        src = P.dr[name]
        shp = list(src.shape)
        src2 = src if len(shp) == 2 else src.rearrange("a b c d -> (a b c) d") if len(shp) == 4 else src.rearrange("a b c -> (a b) c")
        dst = P.dout("dbg_" + name, list(src2.shape), src.dtype)
        n = src2.shape[0]
        for r0 in range(0, n, 512):
            r1 = min(n, r0 + 512)
            op_dma(P.S, dst[r0:r1, :], src2[r0:r1, :], (), ())
    P.S.flush()


def bcast128(v):
    return np.ascontiguousarray(np.broadcast_to(np.asarray(v, np.float32).reshape(1, -1), (128, v.size)))


def host_inputs(cfg, inp, core):
    T = cfg.T
    m = {}
    m["x"] = np.ascontiguousarray(inp["x"][0, core * T:(core + 1) * T, :])
    m["c_ident"] = np.eye(128, dtype=np.float32)
    m["c_utri"] = np.triu(np.ones((128, 128), np.float32))
    ng = np.where(np.arange(128)[:, None] > np.arange(128)[None, :], NEG, 0.0).astype(np.float32)
    m["c_negm4"] = np.ascontiguousarray(np.tile(ng, (1, 4)))
    mr = (np.arange(8) < core).astype(np.float32)
    m["c_mrank"] = np.ascontiguousarray(np.broadcast_to(np.concatenate([mr, 1.0 - mr])[None, :], (128, 16)))
    sel = np.zeros((24, 3), np.float32)
    if core > 0:
        for i in range(3):
            sel[(core - 1) * 3 + i, i] = 1.0
    m["c_sel24"] = sel
    js = jd = 0
    for li, kind in enumerate(cfg.layers):
        m["ln_g%d" % li] = bcast128(inp["ln_g"][li])
        m["ln_b%d" % li] = bcast128(inp["ln_b"][li])
        if kind == 0:
            if cfg.coll:
                ks = D // NCORE
                m["ssd_in_w%d" % js] = np.ascontiguousarray(inp["ssd_in_w"][js, core * ks:(core + 1) * ks, :])
                ks = DI // NCORE
                m["ssd_out_w%d" % js] = np.ascontiguousarray(inp["ssd_out_w"][js, core * ks:(core + 1) * ks, :])
            else:
                m["ssd_in_w%d" % js] = np.ascontiguousarray(inp["ssd_in_w"][js])
                m["ssd_out_w%d" % js] = np.ascontiguousarray(inp["ssd_out_w"][js])
            cw = inp["ssd_conv_w"][js]
            m["ssd_cw%d" % js] = np.ascontiguousarray(cw.reshape(4, 80, 128).transpose(2, 1, 0))
            m["ssd_cb%d" % js] = np.ascontiguousarray(inp["ssd_conv_b"][js].reshape(80, 128).T)
            m["ssd_dtb%d" % js] = bcast128(inp["ssd_dt_bias"][js])
            m["ssd_alog%d" % js] = bcast128(inp["ssd_a_log"][js])
            m["ssd_dsk%d" % js] = bcast128(inp["ssd_d_skip"][js])
            m["ssd_ng%d" % js] = bcast128(inp["ssd_norm_g"][js])
            js += 1
        else:
            dsa_host(cfg, inp, core, jd, m)
            jd += 1
    return m


_CACHE = {}
WCOLL = False


def wshard(w, c):
    if not WCOLL:
        return w
    ks = w.shape[0] // NCORE
    return np.ascontiguousarray(w[c * ks:(c + 1) * ks])


def ssd_scratch(P):
    T, TT = P.cfg.T, P.cfg.TT
    P.dtmp("zs", [T, DI])
    P.dtmp("Xf", [T, DI])
    P.dtmp("Xb", [T, DI], BF16)
    P.dtmp("Bb", [T, 1024], BF16)
    P.dtmp("BT", [1024, T], BF16)
    P.dtmp("CT", [1024, T], BF16)
    P.dtmp("dt", [T, 128])
    P.dtmp("dtA", [T, 128])
    P.dtmp("eacs", [T, 128])
    P.dtmp("cdec", [TT, 128, 128])
    P.dtmp("states", [TT, 8, 128, 1024])
    P.dtmp("y1", [T, DI])
    P.dtmp("Sfin", [128, DI])
    P.dtmp("Sall", [NCORE * 128, DI])
    P.dtmp("Afin", [128, 128])
    P.dtmp("Aall", [NCORE * 128, 128])
    P.dtmp("Hinit", [128, DI])
    P.dtmp("ynT", [DI, T], BF16)
    P.res["Sall"] = Res()
    P.res["Aall"] = Res()


def build_kind(L, kind):
    cfg = Cfg(L=L, layers=(1,) if kind.startswith("dsa") else (0,))
    cfg.coll = False
    cfg.multi = True
    cfg.wcoll = WCOLL
    cfg.kind = kind
    cfg.ext_in = {"ssd_b": ("Sall", "Aall"), "dsa_d": ("kTall", "Vall", "kiTall")}.get(kind, ())
    cfg.ext_out = {"ssd_a": ("Sfin", "Afin"), "dsa_c": ("kT", "V", "kiT")}.get(kind, ())
    P = Prog(cfg)
    T = cfg.T
    P.din("x", [T, D])
    P.dtmp("xres", [T, D])
    P.dtmp("xT", [D, T], BF16)
    P.dtmp("tail", [3, D], BF16)
    P.dtmp("tails", [3 * NCORE, D], BF16)
    P.res["tails"] = Res()
    if kind in ("ssd_b", "dsa_d"):
        P.dout("out", [T, D])
        P.dtmp("pre", [T, D])
        P.din("ln_g0", [128, D])
        P.din("ln_b0", [128, D])
    if kind.startswith("ssd"):
        Win = gather_weight(P, "ssd_in_w0", D, SSD_IN - DI if kind == "ssd_a" else SSD_IN)
        if kind == "ssd_b":
            Wout = gather_weight(P, "ssd_out_w0", DI, D)
        P.din("tails_in", [3 * NCORE, D])
        P.din("ssd_cw0", [128, 80, 4])
        P.din("ssd_cb0", [128, 80])
        for nm in ("dtb", "alog", "dsk"):
            P.din("ssd_%s0" % nm, [128, 128])
        P.din("ssd_ng0", [128, DI])
        ssd_scratch(P)
        phase_weights(P)
        setup_consts(P)
        phase_prologue(P)
        ssd_phaseA(P, 0, Win, col_start=DI if kind == "ssd_a" else 0)
        ssd_phaseB(P, 0)
        if kind == "ssd_b":
            ssd_phaseC(P, 0)
            ssd_phaseD(P, 0)
            phase_outproj(P, P.dr["ynT"], DI, Wout, P.res["ssd_out_w0"], 0, P.dr["x"], True)
    else:
        Win = gather_weight(P, "dsa_in_w0", D, 2176 if kind == "dsa_c" else DSA_IN)
        if kind == "dsa_d":
            Wout = gather_weight(P, "dsa_out_w0", D, D)
        dsa_decls(P)
        phase_weights(P)
        setup_consts(P)
        phase_prologue(P)
        dsa_phaseA(P, 0, Win, kv_only=(kind == "dsa_c"))
        if kind == "dsa_d":
            dsa_phaseB(P, 0)
            phase_outproj(P, P.dr["ogT"], D, Wout, P.res["dsa_out_w0"], 0, P.dr["x"], True)
    return P


def const_inputs(cfg, core, m):
    m["c_ident"] = np.eye(128, dtype=np.float32)
    m["c_utri"] = np.triu(np.ones((128, 128), np.float32))
    ng = np.where(np.arange(128)[:, None] > np.arange(128)[None, :], NEG, 0.0).astype(np.float32)
    m["c_negm4"] = np.ascontiguousarray(np.tile(ng, (1, 4)))
    mr = (np.arange(NCORE) < core).astype(np.float32)
    m["c_mrank"] = np.ascontiguousarray(np.broadcast_to(np.concatenate([mr, 1.0 - mr])[None, :], (128, 2 * NCORE)))
    sel = np.zeros((3 * NCORE, 3), np.float32)
    if core > 0:
        for i in range(3):
            sel[(core - 1) * 3 + i, i] = 1.0
    m["c_sel24"] = sel


def launch(L, kind, maps):
    key = (L, kind)
    if key not in _CACHE:
        _CACHE[key] = build_kind(L, kind)
    P = _CACHE[key]
    res = run_bass_kernel_spmd(P.nc, maps, core_ids=list(range(NCORE)))
    return res.results


def kernel(**inputs):
    inp = {k: np.asarray(v) for k, v in inputs.items()}
    L = inp["x"].shape[1]
    cfg = Cfg(L=L)
    T = cfg.T
    xs = [np.ascontiguousarray(inp["x"][0, c * T:(c + 1) * T, :]) for c in range(NCORE)]
    js = jd = 0
    nlayers = inp["ln_g"].shape[0]
    for li in range(nlayers):
        base = []
        for c in range(NCORE):
            m = {"x": xs[c]}
            const_inputs(cfg, c, m)
            base.append(m)
        lng, lnb = bcast128(inp["ln_g"][li]), bcast128(inp["ln_b"][li])
        if li % 2 == 0:
            j = js
            js += 1
            small = {}
            cw = inp["ssd_conv_w"][j]
            small["ssd_cw0"] = np.ascontiguousarray(cw.reshape(4, 80, 128).transpose(2, 1, 0))
            small["ssd_cb0"] = np.ascontiguousarray(inp["ssd_conv_b"][j].reshape(80, 128).T)
            small["ssd_dtb0"] = bcast128(inp["ssd_dt_bias"][j])
            small["ssd_alog0"] = bcast128(inp["ssd_a_log"][j])
            small["ssd_dsk0"] = bcast128(inp["ssd_d_skip"][j])
            small["ssd_ng0"] = bcast128(inp["ssd_norm_g"][j])
            win_full = np.ascontiguousarray(inp["ssd_in_w"][j])
            win_a = np.ascontiguousarray(inp["ssd_in_w"][j][:, DI:])
            wout = np.ascontiguousarray(inp["ssd_out_w"][j])
            for c in range(NCORE):
                tl = np.zeros((3 * NCORE, D), np.float32)
                if c > 0:
                    tl[(c - 1) * 3:c * 3] = xs[c - 1][-3:]
                base[c]["tails_in"] = tl
                base[c].update(small)
            ra = launch(L, "ssd_a", [dict(base[c], ssd_in_w0=wshard(win_a, c)) for c in range(NCORE)])
            Sall = np.ascontiguousarray(np.concatenate([ra[c]["Sfin"] for c in range(NCORE)], 0))
            Aall = np.ascontiguousarray(np.concatenate([ra[c]["Afin"] for c in range(NCORE)], 0))
            rb = launch(L, "ssd_b", [dict(base[c], ssd_in_w0=wshard(win_full, c), ssd_out_w0=wshard(wout, c), Sall=Sall, Aall=Aall, ln_g0=lng, ln_b0=lnb)
                                     for c in range(NCORE)])
            xs = [np.asarray(rb[c]["out"]) for c in range(NCORE)]
        else:
            j = jd
            jd += 1
            w = inp["dsa_in_w"][j]
            win_c = np.ascontiguousarray(np.concatenate([w[:, KOFF:ZOFF], w[:, KIOFF:KIOFF + 128]], 1))
            win_full = np.ascontiguousarray(w)
            wout = np.ascontiguousarray(inp["dsa_out_w"][j])
            for c in range(NCORE):
                dm = {}
                dsa_tables(cfg, c, dm)
                dm["dsa_kng0"] = bcast128(inp["dsa_kn_g"][j])
                dm["dsa_knb0"] = bcast128(inp["dsa_kn_b"][j])
                base[c].update(dm)
            rc = launch(L, "dsa_c", [dict(base[c], dsa_in_w0=wshard(win_c, c)) for c in range(NCORE)])
            kTall = np.ascontiguousarray(np.concatenate([rc[c]["kT"] for c in range(NCORE)], 0))
            Vall = np.ascontiguousarray(np.concatenate([rc[c]["V"] for c in range(NCORE)], 0))
            kiTall = np.ascontiguousarray(np.concatenate([rc[c]["kiT"] for c in range(NCORE)], 0))
            rd = launch(L, "dsa_d", [dict(base[c], dsa_in_w0=wshard(win_full, c), dsa_out_w0=wshard(wout, c), kTall=kTall, Vall=Vall, kiTall=kiTall,
                                          ln_g0=lng, ln_b0=lnb) for c in range(NCORE)])
            xs = [np.asarray(rd[c]["out"]) for c in range(NCORE)]
    out = np.concatenate(xs, axis=0)
    return out.reshape(1, L, D).astype(np.float32)
```
